# Optimizing a Trainium2 kernel written in Bass

```python
import jax, jax.numpy as jnp
from jax import lax
import numpy as np

D_MODEL = 4096
BATCH = 4
SEQ = 2048
DEPTH = 2
DEC_BATCH = 128
DEC_SEQ = 8
PAST_LEN = 16384
PAGE_SIZE = 128

MIX_WIDTH = D_MODEL
POOL_WIDTH = D_MODEL // 4
SCONV_WIDTH = 3 * D_MODEL // 8
CONF_WIDTH = MIX_WIDTH - POOL_WIDTH - SCONV_WIDTH
POOL_WINDOWS = (2, 4, 8, 16)
POOL_GROUPS = len(POOL_WINDOWS)
POOL_GROUP_WIDTH = POOL_WIDTH // POOL_GROUPS
POOL_CTX = max(POOL_WINDOWS) - 1
SCONV_K = 3
CONF_K = 31
D_FF = ((8 * D_MODEL // 3 + 255) // 256) * 256
IN_COLS = POOL_WIDTH + 3 * SCONV_WIDTH + 2 * CONF_WIDTH
EPS = 1e-6

kernel_name = 'hybrid_pool_shortconv_conformer_decoder_step'


def rmsnorm(x, g):
    xf = x.astype(jnp.float32)
    y = xf * lax.rsqrt(jnp.mean(xf * xf, axis=-1, keepdims=True) + EPS)
    return (y * g.astype(jnp.float32)).astype(x.dtype)


def layernorm(x, g, b):
    xf = x.astype(jnp.float32)
    mu = jnp.mean(xf, axis=-1, keepdims=True)
    var = jnp.mean(jnp.square(xf - mu), axis=-1, keepdims=True)
    y = (xf - mu) * lax.rsqrt(var + EPS)
    return (y * g.astype(jnp.float32) + b.astype(jnp.float32)).astype(x.dtype)


def swiglu(x, w_gate, w_up, w_down):
    return (jax.nn.silu(x @ w_gate) * (x @ w_up)) @ w_down


def causal_dwconv(x, ctx, w):
    K, C = w.shape
    xc = jnp.concatenate([ctx.astype(x.dtype), x], axis=1)
    y = lax.conv_general_dilated(xc, w[:, None, :].astype(x.dtype), window_strides=(1,),
                                 padding='VALID', dimension_numbers=('NWC', 'WIO', 'NWC'),
                                 feature_group_count=C)
    return y, xc[:, xc.shape[1] - (K - 1):]


def pool_mixer(v, ctx, pos0, pool_w, pool_scale):
    B, T, _ = v.shape
    vc = jnp.concatenate([ctx.astype(v.dtype), v], axis=1)
    cs = jnp.cumsum(vc.astype(jnp.float32), axis=1)
    cs = jnp.pad(cs, ((0, 0), (1, 0), (0, 0)))
    pos = pos0 + jnp.arange(T)
    end = cs[:, POOL_CTX + 1:POOL_CTX + 1 + T]
    means = []
    for g, w in enumerate(POOL_WINDOWS):
        sl = slice(g * POOL_GROUP_WIDTH, (g + 1) * POOL_GROUP_WIDTH)
        start = cs[:, POOL_CTX + 1 - w:POOL_CTX + 1 - w + T, sl]
        count = jnp.minimum(pos + 1, w).astype(jnp.float32)[None, :, None]
        means.append((end[..., sl] - start) / count)
    mean = jnp.concatenate(means, axis=-1)
    d = (mean - v.astype(jnp.float32)).astype(v.dtype)
    d = d.reshape(B, T, POOL_GROUPS, POOL_GROUP_WIDTH)
    y = jnp.einsum('btgc,gcd->btgd', d, pool_w).reshape(B, T, POOL_WIDTH) * pool_scale
    return y, vc[:, vc.shape[1] - POOL_CTX:]


def mixer_block(xn, ctx_pool, ctx_sconv, ctx_conf, pos0, w_in, pool_w, pool_scale,
                sconv_w, conf_dw_w, conf_dw_b, conf_ln_g, conf_ln_b, w_out):
    proj = xn @ w_in
    o = 0
    v_pool = proj[..., o:o + POOL_WIDTH]; o += POOL_WIDTH
    b_gate = proj[..., o:o + SCONV_WIDTH]; o += SCONV_WIDTH
    c_gate = proj[..., o:o + SCONV_WIDTH]; o += SCONV_WIDTH
    h_sc = proj[..., o:o + SCONV_WIDTH]; o += SCONV_WIDTH
    a_conf = proj[..., o:o + CONF_WIDTH]; o += CONF_WIDTH
    g_conf = proj[..., o:o + CONF_WIDTH]
    y_pool, new_pool = pool_mixer(v_pool, ctx_pool, pos0, pool_w, pool_scale)
    u = c_gate * h_sc
    conv_u, new_sconv = causal_dwconv(u, ctx_sconv, sconv_w)
    y_sc = b_gate * conv_u
    glu = a_conf * jax.nn.sigmoid(g_conf)
    conv_c, new_conf = causal_dwconv(glu, ctx_conf, conf_dw_w)
    y_conf = jax.nn.silu(layernorm(conv_c + conf_dw_b, conf_ln_g, conf_ln_b))
    y = jnp.concatenate([y_pool, y_sc, y_conf], axis=-1) @ w_out
    return y, new_pool, new_sconv, new_conf


def run_layer(x, ctx_pool, ctx_sconv, ctx_conf, pos0, l,
              ffn1_norm, ffn1_w_gate, ffn1_w_up, ffn1_w_down, mix_norm, w_in, pool_w,
              pool_scale, sconv_w, conf_dw_w, conf_dw_b, conf_ln_g, conf_ln_b, w_out,
              ffn2_norm, ffn2_w_gate, ffn2_w_up, ffn2_w_down):
    x = x + 0.5 * swiglu(rmsnorm(x, ffn1_norm[l]), ffn1_w_gate[l], ffn1_w_up[l], ffn1_w_down[l])
    y, new_pool, new_sconv, new_conf = mixer_block(
        rmsnorm(x, mix_norm[l]), ctx_pool, ctx_sconv, ctx_conf, pos0, w_in[l], pool_w[l],
        pool_scale[l], sconv_w[l], conf_dw_w[l], conf_dw_b[l], conf_ln_g[l], conf_ln_b[l], w_out[l])
    x = x + y
    x = x + 0.5 * swiglu(rmsnorm(x, ffn2_norm[l]), ffn2_w_gate[l], ffn2_w_up[l], ffn2_w_down[l])
    return x, new_pool, new_sconv, new_conf


def setup_inputs(seed: int = 0) -> dict:
    key = jax.random.key(seed)
    ks = jax.random.split(key, 24)
    f32 = jnp.float32

    def nrm(k, shape, scale):
        return jax.random.normal(k, shape, f32) * scale

    def gain(k, shape):
        return 1.0 + 0.02 * jax.random.normal(k, shape, f32)

    return {
        'x_prompt': nrm(ks[0], (BATCH, SEQ, D_MODEL), 1.0),
        'x_sample': nrm(ks[1], (DEC_BATCH, DEC_SEQ, D_MODEL), 1.0),
        'state_pool': nrm(ks[2], (DEPTH, DEC_BATCH, POOL_CTX, POOL_WIDTH), 1.0),
        'state_sconv': nrm(ks[3], (DEPTH, DEC_BATCH, SCONV_K - 1, SCONV_WIDTH), 1.0),
        'state_conf': nrm(ks[4], (DEPTH, DEC_BATCH, CONF_K - 1, CONF_WIDTH), 0.5),
        'ffn1_norm': gain(ks[5], (DEPTH, D_MODEL)),
        'ffn1_w_gate': nrm(ks[6], (DEPTH, D_MODEL, D_FF), D_MODEL ** -0.5),
        'ffn1_w_up': nrm(ks[7], (DEPTH, D_MODEL, D_FF), D_MODEL ** -0.5),
        'ffn1_w_down': nrm(ks[8], (DEPTH, D_FF, D_MODEL), D_FF ** -0.5),
        'mix_norm': gain(ks[9], (DEPTH, D_MODEL)),
        'w_in': nrm(ks[10], (DEPTH, D_MODEL, IN_COLS), D_MODEL ** -0.5),
        'pool_w': nrm(ks[11], (DEPTH, POOL_GROUPS, POOL_GROUP_WIDTH, POOL_GROUP_WIDTH), POOL_GROUP_WIDTH ** -0.5),
        'pool_scale': gain(ks[12], (DEPTH, POOL_WIDTH)),
        'sconv_w': nrm(ks[13], (DEPTH, SCONV_K, SCONV_WIDTH), SCONV_K ** -0.5),
        'conf_dw_w': nrm(ks[14], (DEPTH, CONF_K, CONF_WIDTH), CONF_K ** -0.5),
        'conf_dw_b': nrm(ks[15], (DEPTH, CONF_WIDTH), 0.02),
        'conf_ln_g': gain(ks[16], (DEPTH, CONF_WIDTH)),
        'conf_ln_b': nrm(ks[17], (DEPTH, CONF_WIDTH), 0.02),
        'w_out': nrm(ks[18], (DEPTH, MIX_WIDTH, D_MODEL), MIX_WIDTH ** -0.5),
        'ffn2_norm': gain(ks[19], (DEPTH, D_MODEL)),
        'ffn2_w_gate': nrm(ks[20], (DEPTH, D_MODEL, D_FF), D_MODEL ** -0.5),
        'ffn2_w_up': nrm(ks[21], (DEPTH, D_MODEL, D_FF), D_MODEL ** -0.5),
        'ffn2_w_down': nrm(ks[22], (DEPTH, D_FF, D_MODEL), D_FF ** -0.5),
        'final_norm': gain(ks[23], (D_MODEL,)),
    }


def reference(x_prompt, x_sample, state_pool, state_sconv, state_conf,
              ffn1_norm, ffn1_w_gate, ffn1_w_up, ffn1_w_down, mix_norm, w_in, pool_w,
              pool_scale, sconv_w, conf_dw_w, conf_dw_b, conf_ln_g, conf_ln_b, w_out,
              ffn2_norm, ffn2_w_gate, ffn2_w_up, ffn2_w_down, final_norm):
    weights = (ffn1_norm, ffn1_w_gate, ffn1_w_up, ffn1_w_down, mix_norm, w_in, pool_w,
               pool_scale, sconv_w, conf_dw_w, conf_dw_b, conf_ln_g, conf_ln_b, w_out,
               ffn2_norm, ffn2_w_gate, ffn2_w_up, ffn2_w_down)
    xp, xs = x_prompt, x_sample
    bp = xp.shape[0]
    pp_pool, pp_sconv, pp_conf = [], [], []
    ps_pool, ps_sconv, ps_conf = [], [], []
    for l in range(DEPTH):
        z_pool = jnp.zeros((bp, POOL_CTX, POOL_WIDTH), xp.dtype)
        z_sconv = jnp.zeros((bp, SCONV_K - 1, SCONV_WIDTH), xp.dtype)
        z_conf = jnp.zeros((bp, CONF_K - 1, CONF_WIDTH), xp.dtype)
        xp, np_, ns_, nc_ = run_layer(xp, z_pool, z_sconv, z_conf, 0, l, *weights)
        pp_pool.append(np_); pp_sconv.append(ns_); pp_conf.append(nc_)
        xs, np_, ns_, nc_ = run_layer(xs, state_pool[l], state_sconv[l], state_conf[l],
                                      PAST_LEN, l, *weights)
        ps_pool.append(np_); ps_sconv.append(ns_); ps_conf.append(nc_)
    y_prompt = rmsnorm(xp, final_norm)
    y_sample = rmsnorm(xs, final_norm)
    return (y_prompt, y_sample,
            jnp.stack(pp_pool), jnp.stack(pp_sconv), jnp.stack(pp_conf),
            jnp.stack(ps_pool), jnp.stack(ps_sconv), jnp.stack(ps_conf))
```

```python
import numpy as np
from contextlib import ExitStack
from collections import defaultdict

import concourse.bass as bass
import concourse.mybir as mybir
from concourse.bass_utils import run_bass_kernel_spmd

F32 = mybir.dt.float32
BF16 = mybir.dt.bfloat16
ALU = mybir.AluOpType
AF = mybir.ActivationFunctionType

EPS = 1e-6
DS = 8
POOL_WINDOWS = (2, 4, 8, 16)
PCTX, SCTX, CCTX = 15, 2, 30
SK, CK = 3, 31
HALO = 64
SELF_SYNC = True


class Cfg:
    def __init__(self, D, DFF, NL, P, S, NB, n_cores, batch, seq):
        self.D, self.DFF, self.NL, self.P, self.S, self.NB = D, DFF, NL, P, S, NB
        self.n_cores, self.batch, self.seq = n_cores, batch, seq
        self.KD = D // 128
        self.KF = DFF // 128
        self.PW = D // 4
        self.SW = 3 * D // 8
        self.CW = D - self.PW - self.SW
        self.NPc, self.NSc, self.NCc = self.PW // 128, self.SW // 128, self.CW // 128
        self.GW = self.PW // 4
        self.cpg = self.GW // 128
        self.T = P + S * DS
        self.INC = self.PW + 3 * self.SW + 2 * self.CW
        assert self.GW % 128 == 0 and D % 128 == 0 and DFF % 128 == 0
        o = 0
        self.c_lay = []
        for l in range(NL):
            d = {}
            for nm, n in (("ffn1_norm", self.KD), ("mix_norm", self.KD), ("ffn2_norm", self.KD),
                          ("pool_scale", self.NPc), ("sconv_w", self.NSc * SK),
                          ("conf_dw_w", self.NCc * CK), ("conf_dw_b", self.NCc),
                          ("conf_ln_g", self.NCc), ("conf_ln_b", self.NCc)):
                d[nm] = o
                o += n
            self.c_lay.append(d)
        self.c_final = o
        o += self.KD
        self.c_rc0 = o
        o += self.NPc * 16
        self.NCST = o


FULL = Cfg(D=4096, DFF=11008, NL=2, P=264, S=4, NB=4, n_cores=8, batch=4, seq=2048)


class Sched:
    ENGS = ("pe", "act", "dve", "pool", "sp")

    def __init__(self):
        self.prog = {e: [] for e in self.ENGS}
        self.cnt = defaultdict(int)
        self.lastw = {}
        self.readers = defaultdict(dict)
        self.waited = {e: defaultdict(int) for e in self.ENGS}
        self.sems = {}
        self.units = set()

    def _deps(self, reads, writes):
        d = {}

        def add(u, v):
            if d.get(u, 0) < v:
                d[u] = v
        for r in reads:
            lw = self.lastw.get(r)
            if lw is not None:
                add(*lw)
        for w in writes:
            lw = self.lastw.get(w)
            if lw is not None:
                add(*lw)
            for u, v in self.readers[w].items():
                add(u, v)
        return d

    def _emit_waits(self, eng, d):
        for u, v in d.items():
            if u == eng and (eng == "pe" or not SELF_SYNC):
                continue
            if self.waited[eng][u] >= v:
                continue
            self.waited[eng][u] = v
            self.prog[eng].append(("wait", u, v))

    def _record(self, unit, val, reads, writes):
        for r in reads:
            rd = self.readers[r]
            if rd.get(unit, 0) < val:
                rd[unit] = val
        for w in writes:
            self.lastw[w] = (unit, val)
            self.readers[w] = {}

    def op(self, eng, fn, reads=(), writes=()):
        self._emit_waits(eng, self._deps(reads, writes))
        self.cnt[eng] += 1
        self.units.add(eng)
        self.prog[eng].append(("op", fn, eng, 1))
        self._record(eng, self.cnt[eng], reads, writes)

    def dma(self, q, stream, fn, reads=(), writes=()):
        self._emit_waits(q, self._deps(reads, writes))
        self.cnt[stream] += 16
        self.units.add(stream)
        self.prog[q].append(("op", fn, stream, 16))
        self._record(stream, self.cnt[stream], reads, writes)

    def wait_all(self, eng, units):
        for u in units:
            if self.cnt[u] > 0 and self.waited[eng][u] < self.cnt[u]:
                self.waited[eng][u] = self.cnt[u]
                self.prog[eng].append(("wait", u, self.cnt[u]))

    def replay(self, eng, e):
        for it in self.prog[eng]:
            if it[0] == "wait":
                e.wait_ge(self.sems[it[1]], it[2])
            else:
                it[1](e).then_inc(self.sems[it[2]], it[3])


def _split_groups(n, g):
    ng = (n + g - 1) // g
    base, rem = divmod(n, ng)
    out, s = [], 0
    for i in range(ng):
        sz = base + (1 if i < rem else 0)
        out.append((s, sz))
        s += sz
    return out


def build_program(cfg):
    c = cfg
    T, P, S, KD, KF = c.T, c.P, c.S, c.KD, c.KF
    NL, NB = c.NL, c.NB
    NPc, NSc, NCc = c.NPc, c.NSc, c.NCc
    assert NB % 2 == 0
    nc = bass.Bass("TRN2", target_bir_lowering=False)

    def din(name, shape):
        return nc.dram_tensor(name, list(shape), F32, kind="ExternalInput").ap()

    def dout(name, shape):
        return nc.dram_tensor(name, list(shape), F32, kind="ExternalOutput").ap()

    xin = din("xin", [NB, 128, KD * T])
    cst_d = din("cst", [128, c.NCST])
    stp_d = din("st_pool", [NL, NB, 128, NPc * S * PCTX])
    sts_d = din("st_sconv", [NL, NB, 128, NSc * S * SCTX])
    stc_d = din("st_conf", [NL, NB, 128, NCc * S * CCTX])
    wg = [din(f"ffn{i}_w_gate", [NL, c.D, c.DFF]) for i in (1, 2)]
    wu = [din(f"ffn{i}_w_up", [NL, c.D, c.DFF]) for i in (1, 2)]
    wd = [din(f"ffn{i}_w_down", [NL, c.DFF, c.D]) for i in (1, 2)]
    w_in = din("w_in", [NL, c.D, c.INC])
    w_out = din("w_out", [NL, c.D, c.D])
    pool_w = din("pool_w", [NL, 4 * c.GW, c.GW])
    yout = dout("yout", [NB, 128, KD * T])
    opp = dout("o_pool_p", [NL, NB, 128, NPc * PCTX])
    ops = dout("o_pool_s", [NL, NB, 128, NPc * S * PCTX])
    osp = dout("o_sconv_p", [NL, NB, 128, NSc * SCTX])
    oss = dout("o_sconv_s", [NL, NB, 128, NSc * S * SCTX])
    ocp = dout("o_conf_p", [NL, NB, 128, NCc * CCTX])
    ocs = dout("o_conf_s", [NL, NB, 128, NCc * S * CCTX])

    EXTW = max(CCTX + P + S * (CCTX + DS), 16)
    NSL = (KD + 16 - 1) // 16
    NSTC = NSL * (((NCc + 1) // 2) * 2 + (NPc + 1) // 2 + ((NSc + 1) // 2) * 3 + (KD + 1) // 2)
    wcache = nc.dram_tensor("wcache", [NL, NSTC, 128, 16 * 256], BF16).ap()
    NEXT = 6
    NTT = 10
    NW = 4
    KSL = 16
    SCOLS = 256
    G = 16
    assert 2 * G >= KD and NCc * 4 + NPc * 2 == 2 * KD

    with ExitStack() as es:
        def sb(name, shape, dt=F32):
            return es.enter_context(nc.sbuf_tensor(name, list(shape), dt))

        x = sb("x", [128, 2, KD, T])
        xn = sb("xn", [128, 2, KD, T], BF16)
        abf = sb("abf", [128, 2 * G, T], BF16)
        wring = sb("wring", [128, NW, KSL * SCOLS], BF16)
        ext = sb("ext", [128, NEXT, EXTW])
        tt = sb("tt", [128, NTT, T])
        stp = sb("stp", [128, NPc, S, PCTX])
        sts = sb("sts", [128, NSc, S, SCTX])
        stc = sb("stc", [128, NCc, S, CCTX])
        carp = sb("carp", [128, NL, NPc, PCTX])
        cars = sb("cars", [128, NL, NSc, SCTX])
        carc = sb("carc", [128, NL, NCc, CCTX])
        cst = sb("cstb", [128, c.NCST])
        ones = sb("ones", [128, 128])
        bar = sb("bar", [128, 2])
        ps = es.enter_context(nc.psum_tensor("ps", [128, 8, 512], F32))

        xn1_flat = xn[:, 1, :, :].rearrange("p k t -> p (k t)")
        convc = xn1_flat[:, 0:NCc * T * 2].bitcast(F32).rearrange("p (c t) -> p c t", t=T)
        dbf = xn1_flat[:, NCc * T * 2:NCc * T * 2 + NPc * T].rearrange("p (c t) -> p c t", t=T)

        sc = Sched()
        st = {"ps": 0, "w": 0, "ext": 0, "tt": 0}

        def new_ps():
            i = st["ps"] % 8
            st["ps"] += 1
            return i

        def new_ext():
            i = st["ext"] % NEXT
            st["ext"] += 1
            return i

        def new_tt():
            i = st["tt"] % NTT
            st["tt"] += 1
            return i

        def load_stage(W2d, k0, nk, m0, ncols, cache=None):
            assert nk * ncols <= KSL * SCOLS
            slot = st["w"] % NW
            st["w"] += 1
            n = nk * ncols
            view = wring[:, slot, 0:n].rearrange("p (k m) -> p k m", m=ncols)
            if cache is not None and cache[0] == "use":
                csrc = wcache[cache[1], cache[2], :, 0:n]
                sc.dma("pool", f"w{slot}", lambda e: e.dma_start(out=wring[:, slot, 0:n], in_=csrc),
                       reads=(("wc", cache[1], cache[2]),), writes=(f"w{slot}",))
                return slot, view
            src = W2d[k0 * 128:(k0 + nk) * 128, m0:m0 + ncols].rearrange("(k p) m -> p k m", p=128)
            sc.dma("pool", f"w{slot}", lambda e: e.dma_start(out=view, in_=src), reads=(), writes=(f"w{slot}",))
            if cache is not None:
                cdst = wcache[cache[1], cache[2], :, 0:n]
                sc.dma("sp", f"wb{slot}", lambda e: e.dma_start(out=cdst, in_=wring[:, slot, 0:n]),
                       reads=(f"w{slot}",), writes=(("wc", cache[1], cache[2]),))
            return slot, view

        def mm_multi(W2d, k0, nk, m0, ncols, groups, cctx=None):
            for ka in range(0, nk, KSL):
                kb = min(nk, ka + KSL)
                cache = None
                if cctx is not None:
                    cache = (cctx["mode"], cctx["l"], cctx["i"])
                    cctx["i"] += 1
                slot, view = load_stage(W2d, k0 + ka, kb - ka, m0, ncols, cache=cache)
                for (col, rhs_fn, res_fn, psi) in groups:
                    def fn(e, ka=ka, kb=kb, view=view, col=col, rhs_fn=rhs_fn, psi=psi):
                        ins = None
                        for k in range(ka, kb):
                            ins = e.matmul(ps[:, psi, 0:T], lhsT=view[:, k - ka, col:col + 128], rhs=rhs_fn(k),
                                           start=(k == 0), stop=(k == nk - 1))
                        return ins
                    sc.op("pe", fn, reads=(f"w{slot}",) + tuple(res_fn(k) for k in range(ka, kb)),
                          writes=(f"ps{psi}",))

        def cc(off, n=1):
            return cst[:, off:off + n]

        sc.dma("sp", "cin", lambda e: e.dma_start(out=cst[:, :], in_=cst_d[:, :]), writes=("cst",))
        sc.op("dve", lambda e: e.memset(ones[:, :], 1.0), writes=("ones",))
        sc.op("dve", lambda e: e.memset(carp[:].rearrange("p a b c -> p (a b c)"), 0.0), writes=("carp",))
        sc.op("dve", lambda e: e.memset(cars[:].rearrange("p a b c -> p (a b c)"), 0.0), writes=("cars",))
        sc.op("dve", lambda e: e.memset(carc[:].rearrange("p a b c -> p (a b c)"), 0.0), writes=("carc",))

        def rsqrt_inplace(ti):
            sc.op("act", lambda e: e.activation(out=tt[:, ti, :], in_=tt[:, ti, :], func=AF.Sqrt),
                  reads=(("tt", ti),), writes=(("tt", ti),))
            sc.op("dve", lambda e: e.reciprocal(out=tt[:, ti, :], in_=tt[:, ti, :]),
                  reads=(("tt", ti),), writes=(("tt", ti),))

        def rms_stats(h):
            psi = new_ps()
            for k in range(KD):
                ti = new_tt()
                if k % 2 == 0:
                    sc.op("act", lambda e, k=k, ti=ti: e.activation(out=tt[:, ti, :], in_=x[:, h, k, :], func=AF.Square),
                          reads=(("x", h, k),), writes=(("tt", ti),))
                else:
                    sc.op("dve", lambda e, k=k, ti=ti: e.tensor_tensor(out=tt[:, ti, :], in0=x[:, h, k, :],
                                                                       in1=x[:, h, k, :], op=ALU.mult),
                          reads=(("x", h, k),), writes=(("tt", ti),))
                sc.op("pe", lambda e, k=k, ti=ti: e.matmul(ps[:, psi, 0:T], lhsT=ones[:, :], rhs=tt[:, ti, :],
                                                           start=(k == 0), stop=(k == KD - 1)),
                      reads=("ones", ("tt", ti)), writes=(f"ps{psi}",))
            ri = new_tt()
            sc.op("dve", lambda e: e.tensor_scalar(out=tt[:, ri, :], in0=ps[:, psi, 0:T], scalar1=1.0 / c.D,
                                                   scalar2=EPS, op0=ALU.mult, op1=ALU.add),
                  reads=(f"ps{psi}",), writes=(("tt", ri),))
            rsqrt_inplace(ri)
            return ri

        def rms_to_xn(h, hd, goff):
            ri = rms_stats(h)
            for k in range(KD):
                sc.op("dve", lambda e, k=k: e.scalar_tensor_tensor(out=xn[:, hd, k, :], in0=x[:, h, k, :],
                                                                    scalar=cc(goff + k), in1=tt[:, ri, :],
                                                                    op0=ALU.mult, op1=ALU.mult),
                      reads=(("x", h, k), ("tt", ri), "cst"), writes=(("xn", hd, k),))

        def ffn(l, wi, goff):
            for h in (0, 1):
                rms_to_xn(h, h, goff)
            Wg, Wu, Wd = wg[wi][l], wu[wi][l], wd[wi][l]
            for (f0, gsz) in _split_groups(KF, G):
                j = 0
                while j < gsz:
                    npair = min(2, gsz - j)
                    col = (f0 + j) * 128
                    gq = [(q, h) for h in (0, 1) for q in range(npair)]
                    mm_multi(Wg, 0, KD, col, npair * 128,
                             [(q * 128, (lambda k, h=h: xn[:, h, k, :]), (lambda k, h=h: ("xn", h, k)), q * 2 + h)
                              for (q, h) in gq])
                    sil = {}
                    for (q, h) in gq:
                        ti = new_tt()
                        sil[(q, h)] = ti
                        pg = q * 2 + h
                        sc.op("act", lambda e, pg=pg, ti=ti: e.activation(out=tt[:, ti, :], in_=ps[:, pg, 0:T],
                                                                         func=AF.Silu),
                              reads=(f"ps{pg}",), writes=(("tt", ti),))
                    mm_multi(Wu, 0, KD, col, npair * 128,
                             [(q * 128, (lambda k, h=h: xn[:, h, k, :]), (lambda k, h=h: ("xn", h, k)), 4 + q * 2 + h)
                              for (q, h) in gq])
                    for (q, h) in gq:
                        ti = sil[(q, h)]
                        pu = 4 + q * 2 + h
                        a = (j + q) * 2 + h
                        sc.op("dve", lambda e, pu=pu, ti=ti, a=a: e.tensor_tensor(out=abf[:, a, :], in0=tt[:, ti, :],
                                                                                  in1=ps[:, pu, 0:T], op=ALU.mult),
                              reads=(f"ps{pu}", ("tt", ti)), writes=(("abf", a),))
                    j += npair
                for dp in range(0, KD, 2):
                    npair = min(2, KD - dp)
                    gq = [(q, h, new_ps()) for q in range(npair) for h in (0, 1)]
                    mm_multi(Wd, f0, gsz, dp * 128, npair * 128,
                             [(q * 128, (lambda k, h=h: abf[:, k * 2 + h, :]), (lambda k, h=h: ("abf", k * 2 + h)), pd)
                              for (q, h, pd) in gq])
                    for (q, h, pd) in gq:
                        d = dp + q
                        sc.op("dve", lambda e, pd=pd, d=d, h=h: e.scalar_tensor_tensor(
                            out=x[:, h, d, :], in0=ps[:, pd, 0:T], scalar=0.5, in1=x[:, h, d, :],
                            op0=ALU.mult, op1=ALU.add),
                            reads=(f"ps{pd}", ("x", h, d)), writes=(("x", h, d),))

        def samp_view(ei, ctx):
            base = ctx + P
            return ext[:, ei, base:base + S * (ctx + DS)].rearrange("p (s j) -> p s j", j=ctx + DS)

        def cp(e, eng, o, i):
            return e.copy(out=o, in_=i) if eng == "act" else e.tensor_copy(out=o, in_=i)

        def tok_views(ap2d):
            return ap2d[:, 0:P], ap2d[:, P:T].rearrange("p (s j) -> p s j", j=DS)

        def fill_ctx(eng, ei, ctx, car_ap, st_ap, car_res, st_res):
            sc.op(eng, lambda e: cp(e, eng, ext[:, ei, 0:ctx], car_ap),
                  reads=(car_res,), writes=(("ext", ei),))
            sv = samp_view(ei, ctx)
            sc.op(eng, lambda e: cp(e, eng, sv[:, :, 0:ctx], st_ap),
                  reads=(st_res,), writes=(("ext", ei),))

        def save_state(eng, ei, ctx, car_ap, st_ap, car_res, st_res):
            sc.op(eng, lambda e: cp(e, eng, car_ap, ext[:, ei, P:P + ctx]),
                  reads=(("ext", ei),), writes=(car_res,))
            sv = samp_view(ei, ctx)
            sc.op(eng, lambda e: cp(e, eng, st_ap, sv[:, :, DS:DS + ctx]),
                  reads=(("ext", ei),), writes=(st_res,))

        def dwconv(ei, ctx, K, woff, out_ap, out_res, boff=None, acc2=None):
            accs = [(out_ap, out_res)] + ([acc2] if acc2 is not None else [])
            na = len(accs)
            sv = samp_view(ei, ctx)
            for k in range(K):
                a_ap, a_res = accs[k % na]
                op_, os_ = tok_views(a_ap)
                ipv = ext[:, ei, k:k + P]
                isv = sv[:, :, k:k + DS]
                for (o_, i_) in ((op_, ipv), (os_, isv)):
                    if k < na:
                        if boff is None or k > 0:
                            sc.op("act", lambda e, o_=o_, i_=i_, k=k: e.mul(out=o_, in_=i_, mul=cc(woff + k)),
                                  reads=(("ext", ei), "cst"), writes=(a_res,))
                        else:
                            sc.op("act", lambda e, o_=o_, i_=i_: e.activation(out=o_, in_=i_, func=AF.Identity,
                                                                              bias=cc(boff), scale=cc(woff)),
                                  reads=(("ext", ei), "cst"), writes=(a_res,))
                    else:
                        sc.op("dve", lambda e, o_=o_, i_=i_, k=k: e.scalar_tensor_tensor(
                            out=o_, in0=i_, scalar=cc(woff + k), in1=o_, op0=ALU.mult, op1=ALU.add),
                            reads=(("ext", ei), "cst", a_res), writes=(a_res,))
            if na == 2:
                b_ap, b_res = accs[1]
                sc.op("dve", lambda e: e.tensor_tensor(out=out_ap, in0=out_ap, in1=b_ap, op=ALU.add),
                      reads=(out_res, b_res), writes=(out_res,))

        def win_jobs(Win, col0, npair, cctx=None):
            banks = [new_ps() for _ in range(npair)]
            mm_multi(Win, 0, KD, col0, npair * 128,
                     [(q * 128, (lambda k: xn[:, 0, k, :]), (lambda k: ("xn", 0, k)), banks[q]) for q in range(npair)],
                     cctx=cctx)
            return banks

        def barrier(res):
            sc.op("dve", lambda e: e.memset(bar[:, 0:1], 0.0), writes=tuple(res) + ("bar",))

        def mixer(l, b, h):
            lay = c.c_lay[l]
            rms_to_xn(h, 0, lay["mix_norm"])
            Win = w_in[l]
            cctx = {"mode": "fill" if b == 0 else "use", "l": l, "i": 0}
            sc.dma("sp", "stin_p", lambda e: e.dma_start(out=stp[:].rearrange("p a b c -> p (a b c)"), in_=stp_d[l, b]),
                   writes=("stp",))
            sc.dma("sp", "stin_s", lambda e: e.dma_start(out=sts[:].rearrange("p a b c -> p (a b c)"), in_=sts_d[l, b]),
                   writes=("sts",))
            sc.dma("sp", "stin_c", lambda e: e.dma_start(out=stc[:].rearrange("p a b c -> p (a b c)"), in_=stc_d[l, b]),
                   writes=("stc",))
            MP, MS, MC = 0, NPc, NPc + NSc
            cA = c.PW + 3 * c.SW
            cG = cA + c.CW

            def conf_pair(c0):
                npair = min(2, NCc - c0)
                pgs = win_jobs(Win, cG + c0 * 128, npair, cctx)
                sig = []
                for q in range(npair):
                    pg = pgs[q]
                    ti = new_tt()
                    sig.append(ti)
                    sc.op("act", lambda e, pg=pg, ti=ti: e.activation(out=tt[:, ti, :], in_=ps[:, pg, 0:T],
                                                                     func=AF.Sigmoid),
                          reads=(f"ps{pg}",), writes=(("tt", ti),))
                pas = win_jobs(Win, cA + c0 * 128, npair, cctx)
                for q in range(npair):
                    ch = c0 + q
                    pa = pas[q]
                    ei = new_ext()
                    fill_ctx("act", ei, CCTX, carc[:, l, ch, :], stc[:, ch, :, :], "carc", "stc")
                    sv = samp_view(ei, CCTX)
                    sp_, ss_ = tok_views(tt[:, sig[q], :])
                    pp_, pss_ = tok_views(ps[:, pa, 0:T])
                    sc.op("dve", lambda e, ei=ei, sp_=sp_, pp_=pp_: e.tensor_tensor(
                        out=ext[:, ei, CCTX:CCTX + P], in0=sp_, in1=pp_, op=ALU.mult),
                        reads=(f"ps{pa}", ("tt", sig[q])), writes=(("ext", ei),))
                    sc.op("dve", lambda e, sv=sv, ss_=ss_, pss_=pss_: e.tensor_tensor(
                        out=sv[:, :, CCTX:CCTX + DS], in0=ss_, in1=pss_, op=ALU.mult),
                        reads=(f"ps{pa}", ("tt", sig[q])), writes=(("ext", ei),))
                    save_state("act", ei, CCTX, carc[:, l, ch, :], stc[:, ch, :, :], "carc", "stc")
                    a2 = new_tt()
                    dwconv(ei, CCTX, CK, lay["conf_dw_w"] + ch * CK, convc[:, ch, :], ("convc", ch),
                           boff=lay["conf_dw_b"] + ch, acc2=(tt[:, a2, :], ("tt", a2)))

            def pool_pair(c0):
                npair = min(2, NPc - c0)
                pvs = win_jobs(Win, c0 * 128, npair, cctx)
                for q in range(npair):
                    ch = c0 + q
                    g = ch // c.cpg
                    w = POOL_WINDOWS[g]
                    pv = pvs[q]
                    e0 = new_ext()
                    fill_ctx("act", e0, PCTX, carp[:, l, ch, :], stp[:, ch, :, :], "carp", "stp")
                    s0v = samp_view(e0, PCTX)
                    pp_, pss_ = tok_views(ps[:, pv, 0:T])
                    sc.op("act", lambda e, e0=e0, pp_=pp_: e.copy(out=ext[:, e0, PCTX:PCTX + P], in_=pp_),
                          reads=(f"ps{pv}",), writes=(("ext", e0),))
                    sc.op("act", lambda e, s0v=s0v, pss_=pss_: e.copy(out=s0v[:, :, PCTX:PCTX + DS], in_=pss_),
                          reads=(f"ps{pv}",), writes=(("ext", e0),))
                    save_state("act", e0, PCTX, carp[:, l, ch, :], stp[:, ch, :, :], "carp", "stp")
                    src = e0
                    sh = 1
                    while sh < w:
                        dst = new_ext()
                        lo = 2 * sh - 1
                        ssv = samp_view(src, PCTX)
                        dsv = samp_view(dst, PCTX)
                        sc.op("dve", lambda e, src=src, dst=dst, lo=lo, sh=sh: e.tensor_tensor(
                            out=ext[:, dst, lo:PCTX + P], in0=ext[:, src, lo:PCTX + P],
                            in1=ext[:, src, lo - sh:PCTX + P - sh], op=ALU.add),
                            reads=(("ext", src),), writes=(("ext", dst),))
                        sc.op("dve", lambda e, ssv=ssv, dsv=dsv, lo=lo, sh=sh: e.tensor_tensor(
                            out=dsv[:, :, lo:PCTX + DS], in0=ssv[:, :, lo:PCTX + DS],
                            in1=ssv[:, :, lo - sh:PCTX + DS - sh], op=ALU.add),
                            reads=(("ext", src),), writes=(("ext", dst),))
                        src = dst
                        sh *= 2
                    wsv = samp_view(src, PCTX)
                    dp_, ds_ = tok_views(dbf[:, ch, :])
                    sc.op("dve", lambda e, src=src, e0=e0, dp_=dp_, w=w: e.scalar_tensor_tensor(
                        out=dp_, in0=ext[:, src, PCTX:PCTX + P], scalar=1.0 / w, in1=ext[:, e0, PCTX:PCTX + P],
                        op0=ALU.mult, op1=ALU.subtract),
                        reads=(("ext", src), ("ext", e0)), writes=(("dbf", ch),))
                    sc.op("dve", lambda e, wsv=wsv, s0v=s0v, ds_=ds_, w=w: e.scalar_tensor_tensor(
                        out=ds_, in0=wsv[:, :, PCTX:PCTX + DS], scalar=1.0 / w, in1=s0v[:, :, PCTX:PCTX + DS],
                        op0=ALU.mult, op1=ALU.subtract),
                        reads=(("ext", src), ("ext", e0)), writes=(("dbf", ch),))
                    if b == 0:
                        ti = new_tt()
                        sc.op("dve", lambda e, src=src, ti=ti, ch=ch: e.tensor_tensor(
                            out=tt[:, ti, 0:PCTX], in0=ext[:, src, PCTX:2 * PCTX],
                            in1=cst[:, c.c_rc0 + ch * 16:c.c_rc0 + ch * 16 + PCTX], op=ALU.mult),
                            reads=(("ext", src), "cst"), writes=(("tt", ti),))
                        sc.op("dve", lambda e, e0=e0, ti=ti, ch=ch: e.tensor_tensor(
                            out=dbf[:, ch, 0:PCTX], in0=tt[:, ti, 0:PCTX], in1=ext[:, e0, PCTX:2 * PCTX],
                            op=ALU.subtract),
                            reads=(("tt", ti), ("ext", e0)), writes=(("dbf", ch),))

            cB, cC, cH = c.PW, c.PW + c.SW, c.PW + 2 * c.SW
            def sconv_pair(c0):
                npair = min(2, NSc - c0)
                pcs = win_jobs(Win, cC + c0 * 128, npair, cctx)
                ctmp = []
                for q in range(npair):
                    pc_ = pcs[q]
                    ti = new_tt()
                    ctmp.append(ti)
                    sc.op("act", lambda e, pc_=pc_, ti=ti: e.copy(out=tt[:, ti, :], in_=ps[:, pc_, 0:T]),
                          reads=(f"ps{pc_}",), writes=(("tt", ti),))
                phs = win_jobs(Win, cH + c0 * 128, npair, cctx)
                cu = []
                for q in range(npair):
                    ch = c0 + q
                    ph = phs[q]
                    ei = new_ext()
                    fill_ctx("act", ei, SCTX, cars[:, l, ch, :], sts[:, ch, :, :], "cars", "sts")
                    sv = samp_view(ei, SCTX)
                    cp_, cs_ = tok_views(tt[:, ctmp[q], :])
                    pp_, pss_ = tok_views(ps[:, ph, 0:T])
                    sc.op("dve", lambda e, ei=ei, cp_=cp_, pp_=pp_: e.tensor_tensor(
                        out=ext[:, ei, SCTX:SCTX + P], in0=cp_, in1=pp_, op=ALU.mult),
                        reads=(f"ps{ph}", ("tt", ctmp[q])), writes=(("ext", ei),))
                    sc.op("dve", lambda e, sv=sv, cs_=cs_, pss_=pss_: e.tensor_tensor(
                        out=sv[:, :, SCTX:SCTX + DS], in0=cs_, in1=pss_, op=ALU.mult),
                        reads=(f"ps{ph}", ("tt", ctmp[q])), writes=(("ext", ei),))
                    save_state("act", ei, SCTX, cars[:, l, ch, :], sts[:, ch, :, :], "cars", "sts")
                    ui = new_tt()
                    cu.append(ui)
                    dwconv(ei, SCTX, SK, lay["sconv_w"] + ch * SK, tt[:, ui, :], ("tt", ui))
                pbs = win_jobs(Win, cB + c0 * 128, npair, cctx)
                for q in range(npair):
                    ch = c0 + q
                    pb = pbs[q]
                    sc.op("dve", lambda e, pb=pb, ui=cu[q], ch=ch: e.tensor_tensor(
                        out=abf[:, MS + ch, :], in0=tt[:, ui, :], in1=ps[:, pb, 0:T], op=ALU.mult),
                        reads=(f"ps{pb}", ("tt", cu[q])), writes=(("abf", MS + ch),))

            nr = max((NCc + 1) // 2, (NSc + 1) // 2, (NPc + 1) // 2)
            for r in range(nr):
                if 2 * r < NCc:
                    conf_pair(2 * r)
                if 2 * r < NSc:
                    sconv_pair(2 * r)
                if 2 * r < NPc:
                    pool_pair(2 * r)
            p_sum, p_sq = new_ps(), new_ps()
            for ch in range(NCc):
                ti = new_tt()
                sc.op("pe", lambda e, ch=ch: e.matmul(ps[:, p_sum, 0:T], lhsT=ones[:, :], rhs=convc[:, ch, :],
                                                      start=(ch == 0), stop=(ch == NCc - 1)),
                      reads=("ones", ("convc", ch)), writes=(f"ps{p_sum}",))
                sc.op("act", lambda e, ch=ch, ti=ti: e.activation(out=tt[:, ti, :], in_=convc[:, ch, :], func=AF.Square),
                      reads=(("convc", ch),), writes=(("tt", ti),))
                sc.op("pe", lambda e, ch=ch, ti=ti: e.matmul(ps[:, p_sq, 0:T], lhsT=ones[:, :], rhs=tt[:, ti, :],
                                                             start=(ch == 0), stop=(ch == NCc - 1)),
                      reads=("ones", ("tt", ti)), writes=(f"ps{p_sq}",))
            mi, vi = new_tt(), new_tt()
            sc.op("dve", lambda e: e.tensor_scalar(out=tt[:, mi, :], in0=ps[:, p_sum, 0:T], scalar1=1.0 / c.CW,
                                                   scalar2=None, op0=ALU.mult),
                  reads=(f"ps{p_sum}",), writes=(("tt", mi),))
            sc.op("dve", lambda e: e.tensor_tensor(out=tt[:, vi, :], in0=tt[:, mi, :], in1=tt[:, mi, :], op=ALU.mult),
                  reads=(("tt", mi),), writes=(("tt", vi),))
            sc.op("dve", lambda e: e.scalar_tensor_tensor(out=tt[:, vi, :], in0=ps[:, p_sq, 0:T], scalar=1.0 / c.CW,
                                                          in1=tt[:, vi, :], op0=ALU.mult, op1=ALU.subtract),
                  reads=(f"ps{p_sq}", ("tt", vi)), writes=(("tt", vi),))
            sc.op("dve", lambda e: e.tensor_scalar(out=tt[:, vi, :], in0=tt[:, vi, :], scalar1=EPS, scalar2=None,
                                                   op0=ALU.add),
                  reads=(("tt", vi),), writes=(("tt", vi),))
            rsqrt_inplace(vi)

            for g in range(4):
                banks = [new_ps() for _ in range(c.cpg)]
                mm_multi(pool_w[l], g * c.cpg, c.cpg, 0, c.GW,
                         [(oc * 128, (lambda k, g=g: dbf[:, g * c.cpg + k, :]), (lambda k, g=g: ("dbf", g * c.cpg + k)),
                           banks[oc]) for oc in range(c.cpg)])
                for oc in range(c.cpg):
                    ch = g * c.cpg + oc
                    pj = banks[oc]
                    sc.op("act", lambda e, pj=pj, ch=ch: e.mul(out=abf[:, MP + ch, :], in_=ps[:, pj, 0:T],
                                                             mul=cc(lay["pool_scale"] + ch)),
                          reads=(f"ps{pj}", "cst"), writes=(("abf", MP + ch),))

            for ch in range(NCc):
                sc.op("dve", lambda e, ch=ch: e.tensor_tensor(out=convc[:, ch, :], in0=convc[:, ch, :],
                                                              in1=tt[:, mi, :], op=ALU.subtract),
                      reads=(("convc", ch), ("tt", mi)), writes=(("convc", ch),))
                sc.op("dve", lambda e, ch=ch: e.tensor_tensor(out=convc[:, ch, :], in0=convc[:, ch, :],
                                                              in1=tt[:, vi, :], op=ALU.mult),
                      reads=(("convc", ch), ("tt", vi)), writes=(("convc", ch),))
                sc.op("act", lambda e, ch=ch: e.activation(out=abf[:, MC + ch, :], in_=convc[:, ch, :], func=AF.Silu,
                                                          bias=cc(lay["conf_ln_b"] + ch),
                                                          scale=cc(lay["conf_ln_g"] + ch)),
                      reads=(("convc", ch), "cst"), writes=(("abf", MC + ch),))

            for dp in range(0, KD, 2):
                npair = min(2, KD - dp)
                banks = [new_ps() for _ in range(npair)]
                mm_multi(w_out[l], 0, KD, dp * 128, npair * 128,
                         [(q * 128, (lambda k: abf[:, k, :]), (lambda k: ("abf", k)), banks[q]) for q in range(npair)],
                         cctx=cctx)
                for q in range(npair):
                    d = dp + q
                    po = banks[q]
                    sc.op("dve", lambda e, po=po, d=d: e.tensor_tensor(out=x[:, h, d, :], in0=x[:, h, d, :],
                                                                       in1=ps[:, po, 0:T], op=ALU.add),
                          reads=(f"ps{po}", ("x", h, d)), writes=(("x", h, d),))

            sc.dma("sp", "stout_cp", lambda e: e.dma_start(out=ocp[l, b], in_=carc[:, l, :, :].rearrange("p a b -> p (a b)")),
                   reads=("carc",))
            sc.dma("sp", "stout_cs", lambda e: e.dma_start(out=ocs[l, b], in_=stc[:].rearrange("p a b c -> p (a b c)")),
                   reads=("stc",))

            sc.dma("sp", "stout_pp", lambda e: e.dma_start(out=opp[l, b], in_=carp[:, l, :, :].rearrange("p a b -> p (a b)")),
                   reads=("carp",))
            sc.dma("sp", "stout_ps", lambda e: e.dma_start(out=ops[l, b], in_=stp[:].rearrange("p a b c -> p (a b c)")),
                   reads=("stp",))

            sc.dma("sp", "stout_sp", lambda e: e.dma_start(out=osp[l, b], in_=cars[:, l, :, :].rearrange("p a b -> p (a b)")),
                   reads=("cars",))
            sc.dma("sp", "stout_ss", lambda e: e.dma_start(out=oss[l, b], in_=sts[:].rearrange("p a b c -> p (a b c)")),
                   reads=("sts",))

        XN1 = tuple(("xn", 1, k) for k in range(KD))
        SCR = tuple(("convc", ch) for ch in range(NCc)) + tuple(("dbf", ch) for ch in range(NPc))
        def load_x(b, h):
            sc.dma("sp", f"xin{h}", lambda e: e.dma_start(out=x[:, h, :, :].rearrange("p k t -> p (k t)"), in_=xin[b]),
                   writes=tuple(("x", h, k) for k in range(KD)))

        for sbi in range(NB // 2):
            if sbi == 0:
                for h in (0, 1):
                    load_x(h, h)
            for l in range(NL):
                lay = c.c_lay[l]
                ffn(l, 0, lay["ffn1_norm"])
                barrier(XN1)
                for h in (0, 1):
                    mixer(l, 2 * sbi + h, h)
                barrier(SCR)
                ffn(l, 1, lay["ffn2_norm"])
            for h in (0, 1):
                b = 2 * sbi + h
                ri = rms_stats(h)
                for k in range(KD):
                    sc.op("dve", lambda e, k=k, ri=ri, h=h: e.scalar_tensor_tensor(
                        out=x[:, h, k, :], in0=x[:, h, k, :], scalar=cc(c.c_final + k), in1=tt[:, ri, :],
                        op0=ALU.mult, op1=ALU.mult),
                        reads=(("x", h, k), ("tt", ri), "cst"), writes=(("x", h, k),))
                sc.dma("sp", f"xout{h}", lambda e, b=b, h=h: e.dma_start(out=yout[b],
                                                                      in_=x[:, h, :, :].rearrange("p k t -> p (k t)")),
                       reads=tuple(("x", h, k) for k in range(KD)))
                if sbi + 1 < NB // 2:
                    load_x(2 * (sbi + 1) + h, h)
        sc.wait_all("sp", ["xout0", "xout1", "stout_cp", "stout_cs", "stout_pp", "stout_ps", "stout_sp", "stout_ss"]
                    + [f"wb{i}" for i in range(NW)])

        for u in sorted(sc.units):
            sc.sems[u] = es.enter_context(nc.semaphore(f"s_{u}"))
        with nc.Block() as block:
            @block.tensor
            def _(e):
                sc.replay("pe", e)

            @block.scalar
            def _(e):
                sc.replay("act", e)

            @block.vector
            def _(e):
                sc.replay("dve", e)

            @block.gpsimd
            def _(e):
                sc.replay("pool", e)

            @block.sync
            def _(e):
                sc.replay("sp", e)
    return nc


def _fm(a2d):
    t, f = a2d.shape
    return np.ascontiguousarray(a2d.T.reshape(f // 128, 128, t).transpose(1, 0, 2)).reshape(128, -1)


def _vec(v):
    return np.ascontiguousarray(v.reshape(-1, 128).T)


def _core_tokens(cfg, core):
    bseq, h = divmod(core, 2)
    per = cfg.NB * cfg.P
    start = 0 if h == 0 else cfg.seq - per
    nseq = cfg.NB * cfg.S
    return bseq, h, start, core * nseq


def _run(cfg, inp):
    c = cfg
    f32 = np.float32
    nc = build_program(c)
    cst = np.zeros((128, c.NCST), f32)
    for l in range(c.NL):
        lay = c.c_lay[l]
        for nm in ("ffn1_norm", "mix_norm", "ffn2_norm", "pool_scale", "conf_dw_b", "conf_ln_g", "conf_ln_b"):
            v = _vec(np.asarray(inp[nm][l], f32))
            cst[:, lay[nm]:lay[nm] + v.shape[1]] = v
        sw = np.asarray(inp["sconv_w"][l], f32)
        cst[:, lay["sconv_w"]:lay["sconv_w"] + c.NSc * SK] = \
            sw.T.reshape(c.NSc, 128, SK).transpose(1, 0, 2).reshape(128, -1)
        cw = np.asarray(inp["conf_dw_w"][l], f32)
        cst[:, lay["conf_dw_w"]:lay["conf_dw_w"] + c.NCc * CK] = \
            cw.T.reshape(c.NCc, 128, CK).transpose(1, 0, 2).reshape(128, -1)
    v = _vec(np.asarray(inp["final_norm"], f32))
    cst[:, c.c_final:c.c_final + c.KD] = v
    for ch in range(c.NPc):
        w = POOL_WINDOWS[ch // c.cpg]
        for t in range(16):
            cst[:, c.c_rc0 + ch * 16 + t] = 1.0 / min(t + 1, w)

    shared = {
        "cst": cst,
        "w_in": np.asarray(inp["w_in"], f32), "w_out": np.asarray(inp["w_out"], f32),
        "pool_w": np.asarray(inp["pool_w"], f32).reshape(c.NL, 4 * c.GW, c.GW),
    }
    for i in (1, 2):
        for nm in ("w_gate", "w_up", "w_down"):
            shared[f"ffn{i}_{nm}"] = np.asarray(inp[f"ffn{i}_{nm}"], f32)

    xp = np.asarray(inp["x_prompt"], f32)
    xs = np.asarray(inp["x_sample"], f32)
    spool = np.asarray(inp["state_pool"], f32)
    ssc = np.asarray(inp["state_sconv"], f32)
    scf = np.asarray(inp["state_conf"], f32)

    def st_fm(stt, l, seqs):
        a = stt[l, seqs]
        s_, ctx, C = a.shape
        return np.ascontiguousarray(a.transpose(2, 0, 1).reshape(C // 128, 128, s_, ctx)
                                    .transpose(1, 0, 2, 3)).reshape(128, -1)

    in_maps = []
    for core in range(c.n_cores):
        bseq, h, start, seq0 = _core_tokens(c, core)
        xin = np.empty((c.NB, 128, c.KD * c.T), f32)
        stp = np.empty((c.NL, c.NB, 128, c.NPc * c.S * PCTX), f32)
        sts = np.empty((c.NL, c.NB, 128, c.NSc * c.S * SCTX), f32)
        stc = np.empty((c.NL, c.NB, 128, c.NCc * c.S * CCTX), f32)
        for b in range(c.NB):
            seqs = slice(seq0 + b * c.S, seq0 + (b + 1) * c.S)
            tok = np.concatenate([xp[bseq, start + b * c.P:start + (b + 1) * c.P],
                                  xs[seqs].reshape(c.S * DS, c.D)], axis=0)
            xin[b] = _fm(tok)
            for l in range(c.NL):
                stp[l, b] = st_fm(spool, l, seqs)
                sts[l, b] = st_fm(ssc, l, seqs)
                stc[l, b] = st_fm(scf, l, seqs)
        m = dict(shared)
        m.update({"xin": xin, "st_pool": stp, "st_sconv": sts, "st_conf": stc})
        in_maps.append(m)

    res = run_bass_kernel_spmd(nc, in_maps, core_ids=list(range(c.n_cores)))
    R = res.results

    nsamp = c.n_cores * c.NB * c.S
    y_p = np.empty((c.batch, c.seq, c.D), f32)
    y_s = np.empty((nsamp, DS, c.D), f32)
    n_pp = np.empty((c.NL, c.batch, PCTX, c.PW), f32)
    n_sp = np.empty((c.NL, c.batch, SCTX, c.SW), f32)
    n_cp = np.empty((c.NL, c.batch, CCTX, c.CW), f32)
    n_ps = np.empty((c.NL, nsamp, PCTX, c.PW), f32)
    n_ss = np.empty((c.NL, nsamp, SCTX, c.SW), f32)
    n_cs = np.empty((c.NL, nsamp, CCTX, c.CW), f32)

    def un_fm(a, nchunk, inner):
        return a.reshape(128, nchunk, inner).transpose(2, 1, 0).reshape(inner, nchunk * 128)

    for core in range(c.n_cores):
        bseq, h, start, seq0 = _core_tokens(c, core)
        r = R[core]
        for b in range(c.NB):
            yt = un_fm(np.asarray(r["yout"][b]), c.KD, c.T)
            g0 = start + b * c.P
            lo = 0
            if h == 1:
                lo = min(c.P, max(0, (start + HALO) - g0))
            if lo < c.P:
                y_p[bseq, g0 + lo:g0 + c.P] = yt[lo:c.P]
            y_s[seq0 + b * c.S:seq0 + (b + 1) * c.S] = yt[c.P:].reshape(c.S, DS, c.D)
            for l in range(c.NL):
                sl = slice(seq0 + b * c.S, seq0 + (b + 1) * c.S)
                n_ps[l, sl] = un_fm(np.asarray(r["o_pool_s"][l, b]), c.NPc, c.S * PCTX).reshape(c.S, PCTX, c.PW)
                n_ss[l, sl] = un_fm(np.asarray(r["o_sconv_s"][l, b]), c.NSc, c.S * SCTX).reshape(c.S, SCTX, c.SW)
                n_cs[l, sl] = un_fm(np.asarray(r["o_conf_s"][l, b]), c.NCc, c.S * CCTX).reshape(c.S, CCTX, c.CW)
                if h == 1 and b == c.NB - 1:
                    n_pp[l, bseq] = un_fm(np.asarray(r["o_pool_p"][l, b]), c.NPc, PCTX)
                    n_sp[l, bseq] = un_fm(np.asarray(r["o_sconv_p"][l, b]), c.NSc, SCTX)
                    n_cp[l, bseq] = un_fm(np.asarray(r["o_conf_p"][l, b]), c.NCc, CCTX)
    return (y_p, y_s, n_pp, n_sp, n_cp, n_ps, n_ss, n_cs)


def kernel(**inputs):
    return _run(FULL, inputs)
```

```python
import numpy as np
from contextlib import ExitStack
from collections import defaultdict

import concourse.bass as bass
import concourse.mybir as mybir
from concourse.bass_utils import run_bass_kernel_spmd

F32 = mybir.dt.float32
BF16 = mybir.dt.bfloat16
ALU = mybir.AluOpType
AF = mybir.ActivationFunctionType

EPS = 1e-6
DS = 8
POOL_WINDOWS = (2, 4, 8, 16)
PCTX, SCTX, CCTX = 15, 2, 30
SK, CK = 3, 31
HALO = 64
SELF_SYNC = True


class Cfg:
    def __init__(self, D, DFF, NL, P, S, NB, n_cores, batch, seq):
        self.D, self.DFF, self.NL, self.P, self.S, self.NB = D, DFF, NL, P, S, NB
        self.n_cores, self.batch, self.seq = n_cores, batch, seq
        self.KD = D // 128
        self.KF = DFF // 128
        self.PW = D // 4
        self.SW = 3 * D // 8
        self.CW = D - self.PW - self.SW
        self.NPc, self.NSc, self.NCc = self.PW // 128, self.SW // 128, self.CW // 128
        self.GW = self.PW // 4
        self.cpg = self.GW // 128
        self.T = P + S * DS
        self.INC = self.PW + 3 * self.SW + 2 * self.CW
        assert self.GW % 128 == 0 and D % 128 == 0 and DFF % 128 == 0
        o = 0
        self.c_lay = []
        for l in range(NL):
            d = {}
            for nm, n in (("ffn1_norm", self.KD), ("mix_norm", self.KD), ("ffn2_norm", self.KD),
                          ("pool_scale", self.NPc), ("sconv_w", self.NSc * SK),
                          ("conf_dw_w", self.NCc * CK), ("conf_dw_b", self.NCc),
                          ("conf_ln_g", self.NCc), ("conf_ln_b", self.NCc)):
                d[nm] = o
                o += n
            self.c_lay.append(d)
        self.c_final = o
        o += self.KD
        self.c_rc0 = o
        o += self.NPc * 16
        self.NCST = o


FULL = Cfg(D=4096, DFF=11008, NL=2, P=264, S=4, NB=4, n_cores=8, batch=4, seq=2048)


class Sched:
    ENGS = ("pe", "act", "dve", "pool", "sp")

    def __init__(self):
        self.prog = {e: [] for e in self.ENGS}
        self.cnt = defaultdict(int)
        self.lastw = {}
        self.readers = defaultdict(dict)
        self.waited = {e: defaultdict(int) for e in self.ENGS}
        self.sems = {}
        self.units = set()

    def _deps(self, reads, writes):
        d = {}

        def add(u, v):
            if d.get(u, 0) < v:
                d[u] = v
        for r in reads:
            lw = self.lastw.get(r)
            if lw is not None:
                add(*lw)
        for w in writes:
            lw = self.lastw.get(w)
            if lw is not None:
                add(*lw)
            for u, v in self.readers[w].items():
                add(u, v)
        return d

    def _emit_waits(self, eng, d):
        for u, v in d.items():
            if u == eng and (eng == "pe" or not SELF_SYNC):
                continue
            if self.waited[eng][u] >= v:
                continue
            self.waited[eng][u] = v
            self.prog[eng].append(("wait", u, v))

    def _record(self, unit, val, reads, writes):
        for r in reads:
            rd = self.readers[r]
            if rd.get(unit, 0) < val:
                rd[unit] = val
        for w in writes:
            self.lastw[w] = (unit, val)
            self.readers[w] = {}

    def op(self, eng, fn, reads=(), writes=()):
        self._emit_waits(eng, self._deps(reads, writes))
        self.cnt[eng] += 1
        self.units.add(eng)
        self.prog[eng].append(("op", fn, eng, 1))
        self._record(eng, self.cnt[eng], reads, writes)

    def dma(self, q, stream, fn, reads=(), writes=()):
        self._emit_waits(q, self._deps(reads, writes))
        self.cnt[stream] += 16
        self.units.add(stream)
        self.prog[q].append(("op", fn, stream, 16))
        self._record(stream, self.cnt[stream], reads, writes)

    def wait_all(self, eng, units):
        for u in units:
            if self.cnt[u] > 0 and self.waited[eng][u] < self.cnt[u]:
                self.waited[eng][u] = self.cnt[u]
                self.prog[eng].append(("wait", u, self.cnt[u]))

    def replay(self, eng, e):
        for it in self.prog[eng]:
            if it[0] == "wait":
                e.wait_ge(self.sems[it[1]], it[2])
            else:
                it[1](e).then_inc(self.sems[it[2]], it[3])


def _split_groups(n, g):
    ng = (n + g - 1) // g
    base, rem = divmod(n, ng)
    out, s = [], 0
    for i in range(ng):
        sz = base + (1 if i < rem else 0)
        out.append((s, sz))
        s += sz
    return out


def build_program(cfg):
    c = cfg
    T, P, S, KD, KF = c.T, c.P, c.S, c.KD, c.KF
    NL, NB = c.NL, c.NB
    NPc, NSc, NCc = c.NPc, c.NSc, c.NCc
    assert NB % 2 == 0
    nc = bass.Bass("TRN2", target_bir_lowering=False)

    def din(name, shape):
        return nc.dram_tensor(name, list(shape), F32, kind="ExternalInput").ap()

    def dout(name, shape):
        return nc.dram_tensor(name, list(shape), F32, kind="ExternalOutput").ap()

    xin = din("xin", [NB, 128, KD * T])
    cst_d = din("cst", [128, c.NCST])
    stp_d = din("st_pool", [NL, NB, 128, NPc * S * PCTX])
    sts_d = din("st_sconv", [NL, NB, 128, NSc * S * SCTX])
    stc_d = din("st_conf", [NL, NB, 128, NCc * S * CCTX])
    wg = [din(f"ffn{i}_w_gate", [NL, c.D, c.DFF]) for i in (1, 2)]
    wu = [din(f"ffn{i}_w_up", [NL, c.D, c.DFF]) for i in (1, 2)]
    wd = [din(f"ffn{i}_w_down", [NL, c.DFF, c.D]) for i in (1, 2)]
    w_in = din("w_in", [NL, c.D, c.INC])
    w_out = din("w_out", [NL, c.D, c.D])
    pool_w = din("pool_w", [NL, 4 * c.GW, c.GW])
    yout = dout("yout", [NB, 128, KD * T])
    opp = dout("o_pool_p", [NL, NB, 128, NPc * PCTX])
    ops = dout("o_pool_s", [NL, NB, 128, NPc * S * PCTX])
    osp = dout("o_sconv_p", [NL, NB, 128, NSc * SCTX])
    oss = dout("o_sconv_s", [NL, NB, 128, NSc * S * SCTX])
    ocp = dout("o_conf_p", [NL, NB, 128, NCc * CCTX])
    ocs = dout("o_conf_s", [NL, NB, 128, NCc * S * CCTX])

    EXTW = max(CCTX + P + S * (CCTX + DS), 16)
    NSL = (KD + 16 - 1) // 16
    NSTC = NSL * (((NCc + 1) // 2) * 2 + (NPc + 1) // 2 + ((NSc + 1) // 2) * 3 + (KD + 1) // 2)
    wcache = nc.dram_tensor("wcache", [NL, NSTC, 128, 16 * 256], BF16).ap()
    NEXT = 6
    NTT = 10
    NW = 4
    KSL = 16
    SCOLS = 256
    G = 16
    assert 2 * G >= KD and NCc * 4 + NPc * 2 == 2 * KD

    with ExitStack() as es:
        def sb(name, shape, dt=F32):
            return es.enter_context(nc.sbuf_tensor(name, list(shape), dt))

        x = sb("x", [128, 2, KD, T])
        xn = sb("xn", [128, 2, KD, T], BF16)
        abf = sb("abf", [128, 2 * G, T], BF16)
        wring = sb("wring", [128, NW, KSL * SCOLS], BF16)
        ext = sb("ext", [128, NEXT, EXTW])
        tt = sb("tt", [128, NTT, T])
        stp = sb("stp", [128, NPc, S, PCTX])
        sts = sb("sts", [128, NSc, S, SCTX])
        stc = sb("stc", [128, NCc, S, CCTX])
        carp = sb("carp", [128, NL, NPc, PCTX])
        cars = sb("cars", [128, NL, NSc, SCTX])
        carc = sb("carc", [128, NL, NCc, CCTX])
        cst = sb("cstb", [128, c.NCST])
        ones = sb("ones", [128, 128])
        bar = sb("bar", [128, 2])
        ps = es.enter_context(nc.psum_tensor("ps", [128, 8, 512], F32))

        xn1_flat = xn[:, 1, :, :].rearrange("p k t -> p (k t)")
        convc = xn1_flat[:, 0:NCc * T * 2].bitcast(F32).rearrange("p (c t) -> p c t", t=T)
        dbf = xn1_flat[:, NCc * T * 2:NCc * T * 2 + NPc * T].rearrange("p (c t) -> p c t", t=T)

        sc = Sched()
        st = {"ps": 0, "w": 0, "ext": 0, "tt": 0}

        def new_ps():
            i = st["ps"] % 8
            st["ps"] += 1
            return i

        def new_ext():
            i = st["ext"] % NEXT
            st["ext"] += 1
            return i

        def new_tt():
            i = st["tt"] % NTT
            st["tt"] += 1
            return i

        def load_stage(W2d, k0, nk, m0, ncols, cache=None):
            assert nk * ncols <= KSL * SCOLS
            slot = st["w"] % NW
            st["w"] += 1
            n = nk * ncols
            view = wring[:, slot, 0:n].rearrange("p (k m) -> p k m", m=ncols)
            if cache is not None and cache[0] == "use":
                csrc = wcache[cache[1], cache[2], :, 0:n]
                sc.dma("pool", f"w{slot}", lambda e: e.dma_start(out=wring[:, slot, 0:n], in_=csrc),
                       reads=(("wc", cache[1], cache[2]),), writes=(f"w{slot}",))
                return slot, view
            src = W2d[k0 * 128:(k0 + nk) * 128, m0:m0 + ncols].rearrange("(k p) m -> p k m", p=128)
            sc.dma("pool", f"w{slot}", lambda e: e.dma_start(out=view, in_=src), reads=(), writes=(f"w{slot}",))
            if cache is not None:
                cdst = wcache[cache[1], cache[2], :, 0:n]
                sc.dma("sp", f"wb{slot}", lambda e: e.dma_start(out=cdst, in_=wring[:, slot, 0:n]),
                       reads=(f"w{slot}",), writes=(("wc", cache[1], cache[2]),))
            return slot, view

        def mm_multi(W2d, k0, nk, m0, ncols, groups, cctx=None, tick=None):
            for ka in range(0, nk, KSL):
                kb = min(nk, ka + KSL)
                cache = None
                if cctx is not None:
                    exp = cctx["expect"][cctx["i"]]
                    assert exp[1:] == (k0 + ka, kb - ka, m0, ncols), (exp, k0 + ka, kb - ka, m0, ncols)
                    cache = ("use", cctx["l"], cctx["i"])
                    cctx["i"] += 1
                slot, view = load_stage(W2d, k0 + ka, kb - ka, m0, ncols, cache=cache)
                if tick is not None:
                    tick()
                for (col, rhs_fn, res_fn, psi) in groups:
                    def fn(e, ka=ka, kb=kb, view=view, col=col, rhs_fn=rhs_fn, psi=psi):
                        ins = None
                        for k in range(ka, kb):
                            ins = e.matmul(ps[:, psi, 0:T], lhsT=view[:, k - ka, col:col + 128], rhs=rhs_fn(k),
                                           start=(k == 0), stop=(k == nk - 1))
                        return ins
                    sc.op("pe", fn, reads=(f"w{slot}",) + tuple(res_fn(k) for k in range(ka, kb)),
                          writes=(f"ps{psi}",))

        def cc(off, n=1):
            return cst[:, off:off + n]

        sc.dma("sp", "cin", lambda e: e.dma_start(out=cst[:, :], in_=cst_d[:, :]), writes=("cst",))
        sc.op("dve", lambda e: e.memset(ones[:, :], 1.0), writes=("ones",))
        sc.op("dve", lambda e: e.memset(carp[:].rearrange("p a b c -> p (a b c)"), 0.0), writes=("carp",))
        sc.op("dve", lambda e: e.memset(cars[:].rearrange("p a b c -> p (a b c)"), 0.0), writes=("cars",))
        sc.op("dve", lambda e: e.memset(carc[:].rearrange("p a b c -> p (a b c)"), 0.0), writes=("carc",))

        def rsqrt_inplace(ti):
            sc.op("act", lambda e: e.activation(out=tt[:, ti, :], in_=tt[:, ti, :], func=AF.Sqrt),
                  reads=(("tt", ti),), writes=(("tt", ti),))
            sc.op("dve", lambda e: e.reciprocal(out=tt[:, ti, :], in_=tt[:, ti, :]),
                  reads=(("tt", ti),), writes=(("tt", ti),))

        def rms_stats(h):
            psi = new_ps()
            for k in range(KD):
                ti = new_tt()
                if k % 2 == 0:
                    sc.op("act", lambda e, k=k, ti=ti: e.activation(out=tt[:, ti, :], in_=x[:, h, k, :], func=AF.Square),
                          reads=(("x", h, k),), writes=(("tt", ti),))
                else:
                    sc.op("dve", lambda e, k=k, ti=ti: e.tensor_tensor(out=tt[:, ti, :], in0=x[:, h, k, :],
                                                                       in1=x[:, h, k, :], op=ALU.mult),
                          reads=(("x", h, k),), writes=(("tt", ti),))
                sc.op("pe", lambda e, k=k, ti=ti: e.matmul(ps[:, psi, 0:T], lhsT=ones[:, :], rhs=tt[:, ti, :],
                                                           start=(k == 0), stop=(k == KD - 1)),
                      reads=("ones", ("tt", ti)), writes=(f"ps{psi}",))
            ri = new_tt()
            sc.op("dve", lambda e: e.tensor_scalar(out=tt[:, ri, :], in0=ps[:, psi, 0:T], scalar1=1.0 / c.D,
                                                   scalar2=EPS, op0=ALU.mult, op1=ALU.add),
                  reads=(f"ps{psi}",), writes=(("tt", ri),))
            rsqrt_inplace(ri)
            return ri

        def rms_to_xn(h, hd, goff):
            ri = rms_stats(h)
            for k in range(KD):
                sc.op("dve", lambda e, k=k: e.scalar_tensor_tensor(out=xn[:, hd, k, :], in0=x[:, h, k, :],
                                                                    scalar=cc(goff + k), in1=tt[:, ri, :],
                                                                    op0=ALU.mult, op1=ALU.mult),
                      reads=(("x", h, k), ("tt", ri), "cst"), writes=(("xn", hd, k),))

        def mixer_stage_list(l):
            Win, lst = w_in[l], []
            cA_ = c.PW + 3 * c.SW
            cG_ = cA_ + c.CW
            cB_, cC_, cH_ = c.PW, c.PW + c.SW, c.PW + 2 * c.SW

            def add(W, col0, npair):
                for ka in range(0, KD, KSL):
                    lst.append((W, ka, min(KD, ka + KSL) - ka, col0, npair * 128))
            nr = max((NCc + 1) // 2, (NSc + 1) // 2, (NPc + 1) // 2)
            for r in range(nr):
                c0 = 2 * r
                if c0 < NCc:
                    add(Win, cG_ + c0 * 128, min(2, NCc - c0))
                    add(Win, cA_ + c0 * 128, min(2, NCc - c0))
                if c0 < NSc:
                    add(Win, cC_ + c0 * 128, min(2, NSc - c0))
                    add(Win, cH_ + c0 * 128, min(2, NSc - c0))
                    add(Win, cB_ + c0 * 128, min(2, NSc - c0))
                if c0 < NPc:
                    add(Win, c0 * 128, min(2, NPc - c0))
            for dp in range(0, KD, 2):
                add(w_out[l], dp * 128, min(2, KD - dp))
            assert len(lst) <= NSTC
            return lst

        def make_converter(l, n_host_loads):
            pend = list(enumerate(mixer_stage_list(l)))
            state = {"acc": 0.0, "rate": len(pend) / max(1, n_host_loads)}

            def emit_one():
                idx, (W, k0, nk, m0, ncols) = pend.pop(0)
                load_stage(W, k0, nk, m0, ncols, cache=("fill", l, idx))

            def tick(flush=False):
                state["acc"] += state["rate"]
                while pend and (flush or state["acc"] >= 1.0):
                    state["acc"] -= 1.0
                    emit_one()
            return tick

        def ffn_stage_loads():
            n = 0
            for (f0, gsz) in _split_groups(KF, G):
                n += ((gsz + 1) // 2) * 2 * ((KD + KSL - 1) // KSL)
                n += ((KD + 1) // 2) * ((gsz + KSL - 1) // KSL)
            return n

        def ffn(l, wi, goff, tick=None):
            for h in (0, 1):
                rms_to_xn(h, h, goff)
            Wg, Wu, Wd = wg[wi][l], wu[wi][l], wd[wi][l]
            for (f0, gsz) in _split_groups(KF, G):
                j = 0
                while j < gsz:
                    npair = min(2, gsz - j)
                    col = (f0 + j) * 128
                    gq = [(q, h) for h in (0, 1) for q in range(npair)]
                    mm_multi(Wg, 0, KD, col, npair * 128,
                             [(q * 128, (lambda k, h=h: xn[:, h, k, :]), (lambda k, h=h: ("xn", h, k)), q * 2 + h)
                              for (q, h) in gq], tick=tick)
                    sil = {}
                    for (q, h) in gq:
                        ti = new_tt()
                        sil[(q, h)] = ti
                        pg = q * 2 + h
                        sc.op("act", lambda e, pg=pg, ti=ti: e.activation(out=tt[:, ti, :], in_=ps[:, pg, 0:T],
                                                                         func=AF.Silu),
                              reads=(f"ps{pg}",), writes=(("tt", ti),))
                    mm_multi(Wu, 0, KD, col, npair * 128,
                             [(q * 128, (lambda k, h=h: xn[:, h, k, :]), (lambda k, h=h: ("xn", h, k)), 4 + q * 2 + h)
                              for (q, h) in gq], tick=tick)
                    for (q, h) in gq:
                        ti = sil[(q, h)]
                        pu = 4 + q * 2 + h
                        a = (j + q) * 2 + h
                        sc.op("dve", lambda e, pu=pu, ti=ti, a=a: e.tensor_tensor(out=abf[:, a, :], in0=tt[:, ti, :],
                                                                                  in1=ps[:, pu, 0:T], op=ALU.mult),
                              reads=(f"ps{pu}", ("tt", ti)), writes=(("abf", a),))
                    j += npair
                for dp in range(0, KD, 2):
                    npair = min(2, KD - dp)
                    gq = [(q, h, new_ps()) for q in range(npair) for h in (0, 1)]
                    mm_multi(Wd, f0, gsz, dp * 128, npair * 128,
                             [(q * 128, (lambda k, h=h: abf[:, k * 2 + h, :]), (lambda k, h=h: ("abf", k * 2 + h)), pd)
                              for (q, h, pd) in gq], tick=tick)
                    for (q, h, pd) in gq:
                        d = dp + q
                        sc.op("dve", lambda e, pd=pd, d=d, h=h: e.scalar_tensor_tensor(
                            out=x[:, h, d, :], in0=ps[:, pd, 0:T], scalar=0.5, in1=x[:, h, d, :],
                            op0=ALU.mult, op1=ALU.add),
                            reads=(f"ps{pd}", ("x", h, d)), writes=(("x", h, d),))

            if tick is not None:
                tick(flush=True)

        def samp_view(ei, ctx):
            base = ctx + P
            return ext[:, ei, base:base + S * (ctx + DS)].rearrange("p (s j) -> p s j", j=ctx + DS)

        def cp(e, eng, o, i):
            return e.copy(out=o, in_=i) if eng == "act" else e.tensor_copy(out=o, in_=i)

        def tok_views(ap2d):
            return ap2d[:, 0:P], ap2d[:, P:T].rearrange("p (s j) -> p s j", j=DS)

        def fill_ctx(eng, ei, ctx, car_ap, st_ap, car_res, st_res):
            sc.op(eng, lambda e: cp(e, eng, ext[:, ei, 0:ctx], car_ap),
                  reads=(car_res,), writes=(("ext", ei),))
            sv = samp_view(ei, ctx)
            sc.op(eng, lambda e: cp(e, eng, sv[:, :, 0:ctx], st_ap),
                  reads=(st_res,), writes=(("ext", ei),))

        def save_state(eng, ei, ctx, car_ap, st_ap, car_res, st_res):
            sc.op(eng, lambda e: cp(e, eng, car_ap, ext[:, ei, P:P + ctx]),
                  reads=(("ext", ei),), writes=(car_res,))
            sv = samp_view(ei, ctx)
            sc.op(eng, lambda e: cp(e, eng, st_ap, sv[:, :, DS:DS + ctx]),
                  reads=(("ext", ei),), writes=(st_res,))

        def dwconv(ei, ctx, K, woff, out_ap, out_res, boff=None, acc2=None):
            accs = [(out_ap, out_res)] + ([acc2] if acc2 is not None else [])
            na = len(accs)
            sv = samp_view(ei, ctx)
            for k in range(K):
                a_ap, a_res = accs[k % na]
                op_, os_ = tok_views(a_ap)
                ipv = ext[:, ei, k:k + P]
                isv = sv[:, :, k:k + DS]
                for (o_, i_) in ((op_, ipv), (os_, isv)):
                    if k < na:
                        if boff is None or k > 0:
                            sc.op("act", lambda e, o_=o_, i_=i_, k=k: e.mul(out=o_, in_=i_, mul=cc(woff + k)),
                                  reads=(("ext", ei), "cst"), writes=(a_res,))
                        else:
                            sc.op("act", lambda e, o_=o_, i_=i_: e.activation(out=o_, in_=i_, func=AF.Identity,
                                                                              bias=cc(boff), scale=cc(woff)),
                                  reads=(("ext", ei), "cst"), writes=(a_res,))
                    else:
                        sc.op("dve", lambda e, o_=o_, i_=i_, k=k: e.scalar_tensor_tensor(
                            out=o_, in0=i_, scalar=cc(woff + k), in1=o_, op0=ALU.mult, op1=ALU.add),
                            reads=(("ext", ei), "cst", a_res), writes=(a_res,))
            if na == 2:
                b_ap, b_res = accs[1]
                sc.op("dve", lambda e: e.tensor_tensor(out=out_ap, in0=out_ap, in1=b_ap, op=ALU.add),
                      reads=(out_res, b_res), writes=(out_res,))

        def win_jobs(Win, col0, npair, cctx=None):
            banks = [new_ps() for _ in range(npair)]
            mm_multi(Win, 0, KD, col0, npair * 128,
                     [(q * 128, (lambda k: xn[:, 0, k, :]), (lambda k: ("xn", 0, k)), banks[q]) for q in range(npair)],
                     cctx=cctx)
            return banks

        def barrier(res):
            sc.op("dve", lambda e: e.memset(bar[:, 0:1], 0.0), writes=tuple(res) + ("bar",))

        def mixer(l, b, h):
            lay = c.c_lay[l]
            rms_to_xn(h, 0, lay["mix_norm"])
            Win = w_in[l]
            cctx = {"l": l, "i": 0, "expect": mixer_stage_list(l)}
            sc.dma("sp", "stin_p", lambda e: e.dma_start(out=stp[:].rearrange("p a b c -> p (a b c)"), in_=stp_d[l, b]),
                   writes=("stp",))
            sc.dma("sp", "stin_s", lambda e: e.dma_start(out=sts[:].rearrange("p a b c -> p (a b c)"), in_=sts_d[l, b]),
                   writes=("sts",))
            sc.dma("sp", "stin_c", lambda e: e.dma_start(out=stc[:].rearrange("p a b c -> p (a b c)"), in_=stc_d[l, b]),
                   writes=("stc",))
            MP, MS, MC = 0, NPc, NPc + NSc
            cA = c.PW + 3 * c.SW
            cG = cA + c.CW

            def conf_pair(c0):
                npair = min(2, NCc - c0)
                pgs = win_jobs(Win, cG + c0 * 128, npair, cctx)
                sig = []
                for q in range(npair):
                    pg = pgs[q]
                    ti = new_tt()
                    sig.append(ti)
                    sc.op("act", lambda e, pg=pg, ti=ti: e.activation(out=tt[:, ti, :], in_=ps[:, pg, 0:T],
                                                                     func=AF.Sigmoid),
                          reads=(f"ps{pg}",), writes=(("tt", ti),))
                pas = win_jobs(Win, cA + c0 * 128, npair, cctx)
                for q in range(npair):
                    ch = c0 + q
                    pa = pas[q]
                    ei = new_ext()
                    fill_ctx("act", ei, CCTX, carc[:, l, ch, :], stc[:, ch, :, :], "carc", "stc")
                    sv = samp_view(ei, CCTX)
                    sp_, ss_ = tok_views(tt[:, sig[q], :])
                    pp_, pss_ = tok_views(ps[:, pa, 0:T])
                    sc.op("dve", lambda e, ei=ei, sp_=sp_, pp_=pp_: e.tensor_tensor(
                        out=ext[:, ei, CCTX:CCTX + P], in0=sp_, in1=pp_, op=ALU.mult),
                        reads=(f"ps{pa}", ("tt", sig[q])), writes=(("ext", ei),))
                    sc.op("dve", lambda e, sv=sv, ss_=ss_, pss_=pss_: e.tensor_tensor(
                        out=sv[:, :, CCTX:CCTX + DS], in0=ss_, in1=pss_, op=ALU.mult),
                        reads=(f"ps{pa}", ("tt", sig[q])), writes=(("ext", ei),))
                    save_state("act", ei, CCTX, carc[:, l, ch, :], stc[:, ch, :, :], "carc", "stc")
                    a2 = new_tt()
                    dwconv(ei, CCTX, CK, lay["conf_dw_w"] + ch * CK, convc[:, ch, :], ("convc", ch),
                           boff=lay["conf_dw_b"] + ch, acc2=(tt[:, a2, :], ("tt", a2)))

            def pool_pair(c0):
                npair = min(2, NPc - c0)
                pvs = win_jobs(Win, c0 * 128, npair, cctx)
                for q in range(npair):
                    ch = c0 + q
                    g = ch // c.cpg
                    w = POOL_WINDOWS[g]
                    pv = pvs[q]
                    e0 = new_ext()
                    fill_ctx("act", e0, PCTX, carp[:, l, ch, :], stp[:, ch, :, :], "carp", "stp")
                    s0v = samp_view(e0, PCTX)
                    pp_, pss_ = tok_views(ps[:, pv, 0:T])
                    sc.op("act", lambda e, e0=e0, pp_=pp_: e.copy(out=ext[:, e0, PCTX:PCTX + P], in_=pp_),
                          reads=(f"ps{pv}",), writes=(("ext", e0),))
                    sc.op("act", lambda e, s0v=s0v, pss_=pss_: e.copy(out=s0v[:, :, PCTX:PCTX + DS], in_=pss_),
                          reads=(f"ps{pv}",), writes=(("ext", e0),))
                    save_state("act", e0, PCTX, carp[:, l, ch, :], stp[:, ch, :, :], "carp", "stp")
                    src = e0
                    sh = 1
                    while sh < w:
                        dst = new_ext()
                        lo = 2 * sh - 1
                        ssv = samp_view(src, PCTX)
                        dsv = samp_view(dst, PCTX)
                        sc.op("dve", lambda e, src=src, dst=dst, lo=lo, sh=sh: e.tensor_tensor(
                            out=ext[:, dst, lo:PCTX + P], in0=ext[:, src, lo:PCTX + P],
                            in1=ext[:, src, lo - sh:PCTX + P - sh], op=ALU.add),
                            reads=(("ext", src),), writes=(("ext", dst),))
                        sc.op("dve", lambda e, ssv=ssv, dsv=dsv, lo=lo, sh=sh: e.tensor_tensor(
                            out=dsv[:, :, lo:PCTX + DS], in0=ssv[:, :, lo:PCTX + DS],
                            in1=ssv[:, :, lo - sh:PCTX + DS - sh], op=ALU.add),
                            reads=(("ext", src),), writes=(("ext", dst),))
                        src = dst
                        sh *= 2
                    wsv = samp_view(src, PCTX)
                    dp_, ds_ = tok_views(dbf[:, ch, :])
                    sc.op("dve", lambda e, src=src, e0=e0, dp_=dp_, w=w: e.scalar_tensor_tensor(
                        out=dp_, in0=ext[:, src, PCTX:PCTX + P], scalar=1.0 / w, in1=ext[:, e0, PCTX:PCTX + P],
                        op0=ALU.mult, op1=ALU.subtract),
                        reads=(("ext", src), ("ext", e0)), writes=(("dbf", ch),))
                    sc.op("dve", lambda e, wsv=wsv, s0v=s0v, ds_=ds_, w=w: e.scalar_tensor_tensor(
                        out=ds_, in0=wsv[:, :, PCTX:PCTX + DS], scalar=1.0 / w, in1=s0v[:, :, PCTX:PCTX + DS],
                        op0=ALU.mult, op1=ALU.subtract),
                        reads=(("ext", src), ("ext", e0)), writes=(("dbf", ch),))
                    if b == 0:
                        ti = new_tt()
                        sc.op("dve", lambda e, src=src, ti=ti, ch=ch: e.tensor_tensor(
                            out=tt[:, ti, 0:PCTX], in0=ext[:, src, PCTX:2 * PCTX],
                            in1=cst[:, c.c_rc0 + ch * 16:c.c_rc0 + ch * 16 + PCTX], op=ALU.mult),
                            reads=(("ext", src), "cst"), writes=(("tt", ti),))
                        sc.op("dve", lambda e, e0=e0, ti=ti, ch=ch: e.tensor_tensor(
                            out=dbf[:, ch, 0:PCTX], in0=tt[:, ti, 0:PCTX], in1=ext[:, e0, PCTX:2 * PCTX],
                            op=ALU.subtract),
                            reads=(("tt", ti), ("ext", e0)), writes=(("dbf", ch),))

            cB, cC, cH = c.PW, c.PW + c.SW, c.PW + 2 * c.SW
            def sconv_pair(c0):
                npair = min(2, NSc - c0)
                pcs = win_jobs(Win, cC + c0 * 128, npair, cctx)
                ctmp = []
                for q in range(npair):
                    pc_ = pcs[q]
                    ti = new_tt()
                    ctmp.append(ti)
                    sc.op("act", lambda e, pc_=pc_, ti=ti: e.copy(out=tt[:, ti, :], in_=ps[:, pc_, 0:T]),
                          reads=(f"ps{pc_}",), writes=(("tt", ti),))
                phs = win_jobs(Win, cH + c0 * 128, npair, cctx)
                cu = []
                for q in range(npair):
                    ch = c0 + q
                    ph = phs[q]
                    ei = new_ext()
                    fill_ctx("act", ei, SCTX, cars[:, l, ch, :], sts[:, ch, :, :], "cars", "sts")
                    sv = samp_view(ei, SCTX)
                    cp_, cs_ = tok_views(tt[:, ctmp[q], :])
                    pp_, pss_ = tok_views(ps[:, ph, 0:T])
                    sc.op("dve", lambda e, ei=ei, cp_=cp_, pp_=pp_: e.tensor_tensor(
                        out=ext[:, ei, SCTX:SCTX + P], in0=cp_, in1=pp_, op=ALU.mult),
                        reads=(f"ps{ph}", ("tt", ctmp[q])), writes=(("ext", ei),))
                    sc.op("dve", lambda e, sv=sv, cs_=cs_, pss_=pss_: e.tensor_tensor(
                        out=sv[:, :, SCTX:SCTX + DS], in0=cs_, in1=pss_, op=ALU.mult),
                        reads=(f"ps{ph}", ("tt", ctmp[q])), writes=(("ext", ei),))
                    save_state("act", ei, SCTX, cars[:, l, ch, :], sts[:, ch, :, :], "cars", "sts")
                    ui = new_tt()
                    cu.append(ui)
                    dwconv(ei, SCTX, SK, lay["sconv_w"] + ch * SK, tt[:, ui, :], ("tt", ui))
                pbs = win_jobs(Win, cB + c0 * 128, npair, cctx)
                for q in range(npair):
                    ch = c0 + q
                    pb = pbs[q]
                    sc.op("dve", lambda e, pb=pb, ui=cu[q], ch=ch: e.tensor_tensor(
                        out=abf[:, MS + ch, :], in0=tt[:, ui, :], in1=ps[:, pb, 0:T], op=ALU.mult),
                        reads=(f"ps{pb}", ("tt", cu[q])), writes=(("abf", MS + ch),))

            nr = max((NCc + 1) // 2, (NSc + 1) // 2, (NPc + 1) // 2)
            for r in range(nr):
                if 2 * r < NCc:
                    conf_pair(2 * r)
                if 2 * r < NSc:
                    sconv_pair(2 * r)
                if 2 * r < NPc:
                    pool_pair(2 * r)
            p_sum, p_sq = new_ps(), new_ps()
            for ch in range(NCc):
                ti = new_tt()
                sc.op("pe", lambda e, ch=ch: e.matmul(ps[:, p_sum, 0:T], lhsT=ones[:, :], rhs=convc[:, ch, :],
                                                      start=(ch == 0), stop=(ch == NCc - 1)),
                      reads=("ones", ("convc", ch)), writes=(f"ps{p_sum}",))
                sc.op("act", lambda e, ch=ch, ti=ti: e.activation(out=tt[:, ti, :], in_=convc[:, ch, :], func=AF.Square),
                      reads=(("convc", ch),), writes=(("tt", ti),))
                sc.op("pe", lambda e, ch=ch, ti=ti: e.matmul(ps[:, p_sq, 0:T], lhsT=ones[:, :], rhs=tt[:, ti, :],
                                                             start=(ch == 0), stop=(ch == NCc - 1)),
                      reads=("ones", ("tt", ti)), writes=(f"ps{p_sq}",))
            mi, vi = new_tt(), new_tt()
            sc.op("dve", lambda e: e.tensor_scalar(out=tt[:, mi, :], in0=ps[:, p_sum, 0:T], scalar1=1.0 / c.CW,
                                                   scalar2=None, op0=ALU.mult),
                  reads=(f"ps{p_sum}",), writes=(("tt", mi),))
            sc.op("dve", lambda e: e.tensor_tensor(out=tt[:, vi, :], in0=tt[:, mi, :], in1=tt[:, mi, :], op=ALU.mult),
                  reads=(("tt", mi),), writes=(("tt", vi),))
            sc.op("dve", lambda e: e.scalar_tensor_tensor(out=tt[:, vi, :], in0=ps[:, p_sq, 0:T], scalar=1.0 / c.CW,
                                                          in1=tt[:, vi, :], op0=ALU.mult, op1=ALU.subtract),
                  reads=(f"ps{p_sq}", ("tt", vi)), writes=(("tt", vi),))
            sc.op("dve", lambda e: e.tensor_scalar(out=tt[:, vi, :], in0=tt[:, vi, :], scalar1=EPS, scalar2=None,
                                                   op0=ALU.add),
                  reads=(("tt", vi),), writes=(("tt", vi),))
            rsqrt_inplace(vi)

            for g in range(4):
                banks = [new_ps() for _ in range(c.cpg)]
                mm_multi(pool_w[l], g * c.cpg, c.cpg, 0, c.GW,
                         [(oc * 128, (lambda k, g=g: dbf[:, g * c.cpg + k, :]), (lambda k, g=g: ("dbf", g * c.cpg + k)),
                           banks[oc]) for oc in range(c.cpg)])
                for oc in range(c.cpg):
                    ch = g * c.cpg + oc
                    pj = banks[oc]
                    sc.op("act", lambda e, pj=pj, ch=ch: e.mul(out=abf[:, MP + ch, :], in_=ps[:, pj, 0:T],
                                                             mul=cc(lay["pool_scale"] + ch)),
                          reads=(f"ps{pj}", "cst"), writes=(("abf", MP + ch),))

            for ch in range(NCc):
                sc.op("dve", lambda e, ch=ch: e.tensor_tensor(out=convc[:, ch, :], in0=convc[:, ch, :],
                                                              in1=tt[:, mi, :], op=ALU.subtract),
                      reads=(("convc", ch), ("tt", mi)), writes=(("convc", ch),))
                sc.op("dve", lambda e, ch=ch: e.tensor_tensor(out=convc[:, ch, :], in0=convc[:, ch, :],
                                                              in1=tt[:, vi, :], op=ALU.mult),
                      reads=(("convc", ch), ("tt", vi)), writes=(("convc", ch),))
                sc.op("act", lambda e, ch=ch: e.activation(out=abf[:, MC + ch, :], in_=convc[:, ch, :], func=AF.Silu,
                                                          bias=cc(lay["conf_ln_b"] + ch),
                                                          scale=cc(lay["conf_ln_g"] + ch)),
                      reads=(("convc", ch), "cst"), writes=(("abf", MC + ch),))

            for dp in range(0, KD, 2):
                npair = min(2, KD - dp)
                banks = [new_ps() for _ in range(npair)]
                mm_multi(w_out[l], 0, KD, dp * 128, npair * 128,
                         [(q * 128, (lambda k: abf[:, k, :]), (lambda k: ("abf", k)), banks[q]) for q in range(npair)],
                         cctx=cctx)
                for q in range(npair):
                    d = dp + q
                    po = banks[q]
                    sc.op("dve", lambda e, po=po, d=d: e.tensor_tensor(out=x[:, h, d, :], in0=x[:, h, d, :],
                                                                       in1=ps[:, po, 0:T], op=ALU.add),
                          reads=(f"ps{po}", ("x", h, d)), writes=(("x", h, d),))

            sc.dma("sp", "stout_cp", lambda e: e.dma_start(out=ocp[l, b], in_=carc[:, l, :, :].rearrange("p a b -> p (a b)")),
                   reads=("carc",))
            sc.dma("sp", "stout_cs", lambda e: e.dma_start(out=ocs[l, b], in_=stc[:].rearrange("p a b c -> p (a b c)")),
                   reads=("stc",))

            sc.dma("sp", "stout_pp", lambda e: e.dma_start(out=opp[l, b], in_=carp[:, l, :, :].rearrange("p a b -> p (a b)")),
                   reads=("carp",))
            sc.dma("sp", "stout_ps", lambda e: e.dma_start(out=ops[l, b], in_=stp[:].rearrange("p a b c -> p (a b c)")),
                   reads=("stp",))

            sc.dma("sp", "stout_sp", lambda e: e.dma_start(out=osp[l, b], in_=cars[:, l, :, :].rearrange("p a b -> p (a b)")),
                   reads=("cars",))
            sc.dma("sp", "stout_ss", lambda e: e.dma_start(out=oss[l, b], in_=sts[:].rearrange("p a b c -> p (a b c)")),
                   reads=("sts",))

        XN1 = tuple(("xn", 1, k) for k in range(KD))
        SCR = tuple(("convc", ch) for ch in range(NCc)) + tuple(("dbf", ch) for ch in range(NPc))
        def load_x(b, h):
            sc.dma("sp", f"xin{h}", lambda e: e.dma_start(out=x[:, h, :, :].rearrange("p k t -> p (k t)"), in_=xin[b]),
                   writes=tuple(("x", h, k) for k in range(KD)))

        for sbi in range(NB // 2):
            if sbi == 0:
                for h in (0, 1):
                    load_x(h, h)
            for l in range(NL):
                lay = c.c_lay[l]
                t1 = make_converter(0, ffn_stage_loads()) if (sbi == 0 and l == 0) else None
                ffn(l, 0, lay["ffn1_norm"], tick=t1)
                barrier(XN1)
                for h in (0, 1):
                    mixer(l, 2 * sbi + h, h)
                barrier(SCR)
                t2 = make_converter(l + 1, ffn_stage_loads()) if (sbi == 0 and l + 1 < NL) else None
                ffn(l, 1, lay["ffn2_norm"], tick=t2)
            for h in (0, 1):
                b = 2 * sbi + h
                ri = rms_stats(h)
                for k in range(KD):
                    sc.op("dve", lambda e, k=k, ri=ri, h=h: e.scalar_tensor_tensor(
                        out=x[:, h, k, :], in0=x[:, h, k, :], scalar=cc(c.c_final + k), in1=tt[:, ri, :],
                        op0=ALU.mult, op1=ALU.mult),
                        reads=(("x", h, k), ("tt", ri), "cst"), writes=(("x", h, k),))
                sc.dma("sp", f"xout{h}", lambda e, b=b, h=h: e.dma_start(out=yout[b],
                                                                      in_=x[:, h, :, :].rearrange("p k t -> p (k t)")),
                       reads=tuple(("x", h, k) for k in range(KD)))
                if sbi + 1 < NB // 2:
                    load_x(2 * (sbi + 1) + h, h)
        sc.wait_all("sp", ["xout0", "xout1", "stout_cp", "stout_cs", "stout_pp", "stout_ps", "stout_sp", "stout_ss"]
                    + [f"wb{i}" for i in range(NW)])

        for u in sorted(sc.units):
            sc.sems[u] = es.enter_context(nc.semaphore(f"s_{u}"))
        with nc.Block() as block:
            @block.tensor
            def _(e):
                sc.replay("pe", e)

            @block.scalar
            def _(e):
                sc.replay("act", e)

            @block.vector
            def _(e):
                sc.replay("dve", e)

            @block.gpsimd
            def _(e):
                sc.replay("pool", e)

            @block.sync
            def _(e):
                sc.replay("sp", e)
    return nc


def _fm(a2d):
    t, f = a2d.shape
    return np.ascontiguousarray(a2d.T.reshape(f // 128, 128, t).transpose(1, 0, 2)).reshape(128, -1)


def _vec(v):
    return np.ascontiguousarray(v.reshape(-1, 128).T)


def _core_tokens(cfg, core):
    bseq, h = divmod(core, 2)
    per = cfg.NB * cfg.P
    start = 0 if h == 0 else cfg.seq - per
    nseq = cfg.NB * cfg.S
    return bseq, h, start, core * nseq


def _run(cfg, inp):
    c = cfg
    f32 = np.float32
    nc = build_program(c)
    cst = np.zeros((128, c.NCST), f32)
    for l in range(c.NL):
        lay = c.c_lay[l]
        for nm in ("ffn1_norm", "mix_norm", "ffn2_norm", "pool_scale", "conf_dw_b", "conf_ln_g", "conf_ln_b"):
            v = _vec(np.asarray(inp[nm][l], f32))
            cst[:, lay[nm]:lay[nm] + v.shape[1]] = v
        sw = np.asarray(inp["sconv_w"][l], f32)
        cst[:, lay["sconv_w"]:lay["sconv_w"] + c.NSc * SK] = \
            sw.T.reshape(c.NSc, 128, SK).transpose(1, 0, 2).reshape(128, -1)
        cw = np.asarray(inp["conf_dw_w"][l], f32)
        cst[:, lay["conf_dw_w"]:lay["conf_dw_w"] + c.NCc * CK] = \
            cw.T.reshape(c.NCc, 128, CK).transpose(1, 0, 2).reshape(128, -1)
    v = _vec(np.asarray(inp["final_norm"], f32))
    cst[:, c.c_final:c.c_final + c.KD] = v
    for ch in range(c.NPc):
        w = POOL_WINDOWS[ch // c.cpg]
        for t in range(16):
            cst[:, c.c_rc0 + ch * 16 + t] = 1.0 / min(t + 1, w)

    shared = {
        "cst": cst,
        "w_in": np.asarray(inp["w_in"], f32), "w_out": np.asarray(inp["w_out"], f32),
        "pool_w": np.asarray(inp["pool_w"], f32).reshape(c.NL, 4 * c.GW, c.GW),
    }
    for i in (1, 2):
        for nm in ("w_gate", "w_up", "w_down"):
            shared[f"ffn{i}_{nm}"] = np.asarray(inp[f"ffn{i}_{nm}"], f32)

    xp = np.asarray(inp["x_prompt"], f32)
    xs = np.asarray(inp["x_sample"], f32)
    spool = np.asarray(inp["state_pool"], f32)
    ssc = np.asarray(inp["state_sconv"], f32)
    scf = np.asarray(inp["state_conf"], f32)

    def st_fm(stt, l, seqs):
        a = stt[l, seqs]
        s_, ctx, C = a.shape
        return np.ascontiguousarray(a.transpose(2, 0, 1).reshape(C // 128, 128, s_, ctx)
                                    .transpose(1, 0, 2, 3)).reshape(128, -1)

    in_maps = []
    for core in range(c.n_cores):
        bseq, h, start, seq0 = _core_tokens(c, core)
        xin = np.empty((c.NB, 128, c.KD * c.T), f32)
        stp = np.empty((c.NL, c.NB, 128, c.NPc * c.S * PCTX), f32)
        sts = np.empty((c.NL, c.NB, 128, c.NSc * c.S * SCTX), f32)
        stc = np.empty((c.NL, c.NB, 128, c.NCc * c.S * CCTX), f32)
        for b in range(c.NB):
            seqs = slice(seq0 + b * c.S, seq0 + (b + 1) * c.S)
            tok = np.concatenate([xp[bseq, start + b * c.P:start + (b + 1) * c.P],
                                  xs[seqs].reshape(c.S * DS, c.D)], axis=0)
            xin[b] = _fm(tok)
            for l in range(c.NL):
                stp[l, b] = st_fm(spool, l, seqs)
                sts[l, b] = st_fm(ssc, l, seqs)
                stc[l, b] = st_fm(scf, l, seqs)
        m = dict(shared)
        m.update({"xin": xin, "st_pool": stp, "st_sconv": sts, "st_conf": stc})
        in_maps.append(m)

    res = run_bass_kernel_spmd(nc, in_maps, core_ids=list(range(c.n_cores)))
    R = res.results

    nsamp = c.n_cores * c.NB * c.S
    y_p = np.empty((c.batch, c.seq, c.D), f32)
    y_s = np.empty((nsamp, DS, c.D), f32)
    n_pp = np.empty((c.NL, c.batch, PCTX, c.PW), f32)
    n_sp = np.empty((c.NL, c.batch, SCTX, c.SW), f32)
    n_cp = np.empty((c.NL, c.batch, CCTX, c.CW), f32)
    n_ps = np.empty((c.NL, nsamp, PCTX, c.PW), f32)
    n_ss = np.empty((c.NL, nsamp, SCTX, c.SW), f32)
    n_cs = np.empty((c.NL, nsamp, CCTX, c.CW), f32)

    def un_fm(a, nchunk, inner):
        return a.reshape(128, nchunk, inner).transpose(2, 1, 0).reshape(inner, nchunk * 128)

    for core in range(c.n_cores):
        bseq, h, start, seq0 = _core_tokens(c, core)
        r = R[core]
        for b in range(c.NB):
            yt = un_fm(np.asarray(r["yout"][b]), c.KD, c.T)
            g0 = start + b * c.P
            lo = 0
            if h == 1:
                lo = min(c.P, max(0, (start + HALO) - g0))
            if lo < c.P:
                y_p[bseq, g0 + lo:g0 + c.P] = yt[lo:c.P]
            y_s[seq0 + b * c.S:seq0 + (b + 1) * c.S] = yt[c.P:].reshape(c.S, DS, c.D)
            for l in range(c.NL):
                sl = slice(seq0 + b * c.S, seq0 + (b + 1) * c.S)
                n_ps[l, sl] = un_fm(np.asarray(r["o_pool_s"][l, b]), c.NPc, c.S * PCTX).reshape(c.S, PCTX, c.PW)
                n_ss[l, sl] = un_fm(np.asarray(r["o_sconv_s"][l, b]), c.NSc, c.S * SCTX).reshape(c.S, SCTX, c.SW)
                n_cs[l, sl] = un_fm(np.asarray(r["o_conf_s"][l, b]), c.NCc, c.S * CCTX).reshape(c.S, CCTX, c.CW)
                if h == 1 and b == c.NB - 1:
                    n_pp[l, bseq] = un_fm(np.asarray(r["o_pool_p"][l, b]), c.NPc, PCTX)
                    n_sp[l, bseq] = un_fm(np.asarray(r["o_sconv_p"][l, b]), c.NSc, SCTX)
                    n_cp[l, bseq] = un_fm(np.asarray(r["o_conf_p"][l, b]), c.NCc, CCTX)
    return (y_p, y_s, n_pp, n_sp, n_cp, n_ps, n_ss, n_cs)


def kernel(**inputs):
    return _run(FULL, inputs)
```
